# Optimizing a Trainium2 kernel written in Bass

```python
import jax, jax.numpy as jnp
from jax import lax
import numpy as np

D_MODEL = 1024
BATCH = 8
SEQ = 4096
DEPTH = 4

N_MIXERS = 3
SHORT_CONV_K = 3
CONFORMER_K = 31
POOL_WINDOWS = (2, 4, 8, 16)
POOL_GROUP = D_MODEL // len(POOL_WINDOWS)
D_FF = ((8 * D_MODEL // 3 + 255) // 256) * 256
N_A = len(range(0, DEPTH, N_MIXERS))
N_B = len(range(1, DEPTH, N_MIXERS))
N_C = len(range(2, DEPTH, N_MIXERS))
EPS = 1e-6

kernel_name = "hybrid_conv_pool_interleaved_adaln"


def rmsnorm(x, g):
    xf = x.astype(jnp.float32)
    y = xf * lax.rsqrt(jnp.mean(xf * xf, axis=-1, keepdims=True) + EPS)
    return (y * g.astype(jnp.float32)).astype(x.dtype)


def layernorm(x, g, b):
    xf = x.astype(jnp.float32)
    mu = jnp.mean(xf, axis=-1, keepdims=True)
    var = jnp.mean(jnp.square(xf - mu), axis=-1, keepdims=True)
    y = (xf - mu) * lax.rsqrt(var + EPS)
    return (y * g.astype(jnp.float32) + b.astype(jnp.float32)).astype(x.dtype)


def causal_dwconv(u, w):
    k = w.shape[0]
    return lax.conv_general_dilated(
        u, w[:, None, :].astype(u.dtype), window_strides=(1,), padding=[(k - 1, 0)],
        dimension_numbers=("NWC", "WIO", "NWC"), feature_group_count=u.shape[-1])


def short_gated_conv(h, w_in, conv_w, w_out):
    proj = jnp.einsum("bsd,de->bse", h, w_in)
    gb, gc, hv = jnp.split(proj, 3, axis=-1)
    v = causal_dwconv(gc * hv, conv_w)
    return jnp.einsum("bsd,de->bse", gb * v, w_out)


def conformer_conv(h, w_pw1, b_pw1, conv_w, conv_b, ln_g, ln_b, w_pw2, b_pw2):
    p = jnp.einsum("bsd,de->bse", h, w_pw1) + b_pw1
    a, gt = jnp.split(p, 2, axis=-1)
    u = a * jax.nn.sigmoid(gt)
    v = causal_dwconv(u, conv_w) + conv_b
    v = jax.nn.silu(layernorm(v, ln_g, ln_b))
    return jnp.einsum("bsd,de->bse", v, w_pw2) + b_pw2


def multiscale_pool(h, w_grp, scale):
    bsz, s, d = h.shape
    hf = h.astype(jnp.float32)
    cs_pad = jnp.concatenate([jnp.zeros((bsz, 1, d), jnp.float32), jnp.cumsum(hf, axis=1)], axis=1)
    t = jnp.arange(s)
    outs = []
    for g, w in enumerate(POOL_WINDOWS):
        sl = slice(g * POOL_GROUP, (g + 1) * POOL_GROUP)
        cg = cs_pad[:, :, sl]
        prev = jnp.concatenate([jnp.zeros((bsz, w - 1, POOL_GROUP), jnp.float32), cg[:, : s - w + 1]], axis=1)
        cnt = jnp.minimum(t + 1, w).astype(jnp.float32)[None, :, None]
        pooled = (cg[:, 1:] - prev) / cnt - hf[:, :, sl]
        outs.append(jnp.einsum("bsg,gh->bsh", pooled.astype(h.dtype), w_grp[g]))
    return jnp.concatenate(outs, axis=-1) * scale


def swiglu(h, w_in, w_out):
    gate, up = jnp.split(jnp.einsum("bsd,df->bsf", h, w_in), 2, axis=-1)
    return jnp.einsum("bsf,fd->bsd", jax.nn.silu(gate) * up, w_out)


def setup_inputs(seed: int = 0) -> dict:
    key = jax.random.key(seed)
    ks = jax.random.split(key, 24)
    D, F, G = D_MODEL, D_FF, POOL_GROUP
    nrm = lambda k, shape, s: jax.random.normal(k, shape, jnp.float32) * s
    return {
        "x": nrm(ks[0], (BATCH, SEQ, D), 1.0),
        "c": nrm(ks[1], (BATCH, D), 1.0),
        "ada_w": nrm(ks[2], (DEPTH, D, 6 * D), 0.5 * D ** -0.5),
        "ada_b": nrm(ks[3], (DEPTH, 6 * D), 0.02),
        "norm_mix_g": 1.0 + nrm(ks[4], (DEPTH, D), 0.02),
        "norm_ffn_g": 1.0 + nrm(ks[5], (DEPTH, D), 0.02),
        "a_w_in": nrm(ks[6], (N_A, D, 3 * D), D ** -0.5),
        "a_conv_w": nrm(ks[7], (N_A, SHORT_CONV_K, D), SHORT_CONV_K ** -0.5),
        "a_w_out": nrm(ks[8], (N_A, D, D), D ** -0.5),
        "b_w_pw1": nrm(ks[9], (N_B, D, 2 * D), D ** -0.5),
        "b_b_pw1": nrm(ks[10], (N_B, 2 * D), 0.02),
        "b_conv_w": nrm(ks[11], (N_B, CONFORMER_K, D), CONFORMER_K ** -0.5),
        "b_conv_b": nrm(ks[12], (N_B, D), 0.02),
        "b_ln_g": 1.0 + nrm(ks[13], (N_B, D), 0.02),
        "b_ln_b": nrm(ks[14], (N_B, D), 0.02),
        "b_w_pw2": nrm(ks[15], (N_B, D, D), D ** -0.5),
        "b_b_pw2": nrm(ks[16], (N_B, D), 0.02),
        "p_w_grp": nrm(ks[17], (N_C, len(POOL_WINDOWS), G, G), G ** -0.5),
        "p_scale": 1.0 + nrm(ks[18], (N_C, D), 0.1),
        "ffn_w_in": nrm(ks[19], (DEPTH, D, 2 * F), D ** -0.5),
        "ffn_w_out": nrm(ks[20], (DEPTH, F, D), F ** -0.5),
        "final_g": 1.0 + nrm(ks[21], (D,), 0.02),
    }


def reference(x, c, ada_w, ada_b, norm_mix_g, norm_ffn_g, a_w_in, a_conv_w, a_w_out,
              b_w_pw1, b_b_pw1, b_conv_w, b_conv_b, b_ln_g, b_ln_b, b_w_pw2, b_b_pw2,
              p_w_grp, p_scale, ffn_w_in, ffn_w_out, final_g):
    c_act = jax.nn.silu(c)
    for i in range(DEPTH):
        kind, j = i % N_MIXERS, i // N_MIXERS
        mod = jnp.einsum("bd,de->be", c_act, ada_w[i]) + ada_b[i]
        sh1, sc1, g1, sh2, sc2, g2 = [m[:, None, :] for m in jnp.split(mod, 6, axis=-1)]
        h = rmsnorm(x, norm_mix_g[i]) * (1.0 + sc1) + sh1
        if kind == 0:
            y = short_gated_conv(h, a_w_in[j], a_conv_w[j], a_w_out[j])
        elif kind == 1:
            y = conformer_conv(h, b_w_pw1[j], b_b_pw1[j], b_conv_w[j], b_conv_b[j],
                               b_ln_g[j], b_ln_b[j], b_w_pw2[j], b_b_pw2[j])
        else:
            y = multiscale_pool(h, p_w_grp[j], p_scale[j])
        x = x + g1 * y
        h = rmsnorm(x, norm_ffn_g[i]) * (1.0 + sc2) + sh2
        x = x + g2 * swiglu(h, ffn_w_in[i], ffn_w_out[i])
    return rmsnorm(x, final_g)
```

```python
import contextlib
import types
import numpy as np
import concourse.bass as bass
import concourse.mybir as mybir
from concourse.bass_utils import run_bass_kernel_spmd

F32 = mybir.dt.float32
BF16 = mybir.dt.bfloat16
ALU = mybir.AluOpType
AF = mybir.ActivationFunctionType

D = 1024
S = 4096
NT = 1024
NSUP = S // NT
F = 2816
FC = F // 128
EPS = 1e-6
NRING = 4
SLOT = 4096
PREFETCH = 3
NLAYERS = 4


def _vec_layout():
    off = {}
    n = 0

    def add(name, w=8):
        nonlocal n
        off[name] = n
        n += w

    for i in range(4):
        add(f"nmg{i}")
        add(f"nfg{i}")
        add(f"adab{i}", 48)
    for j in range(2):
        add(f"acw{j}", 24)
    add("bb1", 16)
    add("bcw", 248)
    add("bcb")
    add("blg")
    add("blb")
    add("bb2")
    add("psc")
    add("fg")
    add("invcnt", 64)
    add("ct", 8)
    return off, n


VOFF, NV = _vec_layout()


def _layer_slabs(i):
    kind = i % 3
    if kind == 0:
        mixer = [("ain", 3072)] * 8 + [("aout", 4096)] * 2
    elif kind == 1:
        mixer = [("bpw1", 4096)] * 4 + [("bpw2", 4096)] * 2
    else:
        mixer = [("cgrp", 2048)]
    ffn = [("fin", 4096)] * 11 + [("fout", 2816)] * 8
    return mixer, ffn


def _slab_table():
    tab = []
    off = 0
    for i in range(4):
        mixer, ffn = _layer_slabs(i)
        m0 = off
        ml = []
        for _, L in mixer:
            ml.append((off, L))
            off += L
        f0 = off
        fl = []
        for _, L in ffn:
            fl.append((off, L))
            off += L
        tab.append(dict(mixer=ml, ffn=fl, mreg=(m0, f0), freg=(f0, off)))
    return tab, off


SLABS, TOTL = _slab_table()


def _colvec(v):
    v = np.asarray(v, np.float32).reshape(-1, 128)
    return np.ascontiguousarray(v.T)


def _pack_cols(w, cols):
    K = w.shape[0]
    sub = w[:, cols]
    n = sub.shape[1]
    return sub.reshape(K // 128, 128, n).transpose(1, 0, 2).reshape(128, (K // 128) * n)


def _pack_weights(inp):
    ar = np.arange
    parts = []
    for i in range(4):
        kind, j = i % 3, i // 3
        if kind == 0:
            w = inp["a_w_in"][j]
            for c in range(8):
                cols = np.concatenate([1024 + c * 128 + ar(128), 2048 + c * 128 + ar(128), c * 128 + ar(128)])
                parts.append(_pack_cols(w, cols))
            w = inp["a_w_out"][j]
            for m in range(2):
                parts.append(_pack_cols(w, m * 512 + ar(512)))
        elif kind == 1:
            w = inp["b_w_pw1"][j]
            for m in range(4):
                cols = np.concatenate([(2 * m) * 128 + ar(128), 1024 + (2 * m) * 128 + ar(128),
                                       (2 * m + 1) * 128 + ar(128), 1024 + (2 * m + 1) * 128 + ar(128)])
                parts.append(_pack_cols(w, cols))
            w = inp["b_w_pw2"][j]
            for m in range(2):
                parts.append(_pack_cols(w, m * 512 + ar(512)))
        else:
            w = inp["p_w_grp"][j]
            parts.append(np.concatenate([_pack_cols(w[g], ar(256)) for g in range(4)], axis=1))
        w = inp["ffn_w_in"][i]
        for m in range(11):
            cols = np.concatenate([(2 * m) * 128 + ar(128), F + (2 * m) * 128 + ar(128),
                                   (2 * m + 1) * 128 + ar(128), F + (2 * m + 1) * 128 + ar(128)])
            parts.append(_pack_cols(w, cols))
        w = inp["ffn_w_out"][i]
        for c in range(8):
            parts.append(_pack_cols(w, c * 128 + ar(128)))
    out = np.ascontiguousarray(np.concatenate(parts, axis=1), dtype=np.float32)
    assert out.shape == (128, TOTL), out.shape
    return out


def _pack_vecs(inp, b):
    v = np.zeros((128, NV), np.float32)

    def put(name, arr):
        arr = np.asarray(arr, np.float32)
        v[:, VOFF[name]:VOFF[name] + arr.shape[1]] = arr

    for i in range(4):
        put(f"nmg{i}", _colvec(inp["norm_mix_g"][i]))
        put(f"nfg{i}", _colvec(inp["norm_ffn_g"][i]))
        put(f"adab{i}", _colvec(inp["ada_b"][i]))
    for j in range(2):
        put(f"acw{j}", np.concatenate([_colvec(inp["a_conv_w"][j][k]) for k in range(3)], axis=1))
    put("bb1", _colvec(inp["b_b_pw1"][0]))
    put("bcw", np.concatenate([_colvec(inp["b_conv_w"][0][k]) for k in range(31)], axis=1))
    put("bcb", _colvec(inp["b_conv_b"][0]))
    put("blg", _colvec(inp["b_ln_g"][0]))
    put("blb", _colvec(inp["b_ln_b"][0]))
    put("bb2", _colvec(inp["b_b_pw2"][0]))
    put("psc", _colvec(inp["p_scale"][0]))
    put("fg", _colvec(inp["final_g"]))
    ic = np.zeros((4, 16), np.float32)
    for g, w in enumerate((2, 4, 8, 16)):
        ic[g] = 1.0 / np.minimum(np.arange(16) + 1, w)
    put("invcnt", np.broadcast_to(ic.reshape(1, 64), (128, 64)))
    put("ct", _colvec(inp["c"][b]))
    return v


class Buf:
    __slots__ = ("name", "w", "r")

    def __init__(self, name):
        self.name = name
        self.w = {}
        self.r = {}


def _snapshot(fn):
    if fn is None or fn.__closure__ is None:
        return fn
    cells = []
    for c in fn.__closure__:
        try:
            v = c.cell_contents
            if isinstance(v, types.FunctionType):
                v = _snapshot(v)
            cells.append(types.CellType(v))
        except ValueError:
            cells.append(c)
    return types.FunctionType(fn.__code__, fn.__globals__, fn.__name__, fn.__defaults__, tuple(cells))


class Prog:
    ENGS = ("pe", "act", "dve", "pool", "sp")

    def __init__(self):
        self.q = {e: [] for e in self.ENGS}
        self.cnt = {}
        self.seen = {e: {} for e in self.ENGS}

    def op(self, eng, fn, reads=(), writes=(), sem=None, inc=1):
        need = {}
        for b in reads:
            for k, v in b.w.items():
                if need.get(k, 0) < v:
                    need[k] = v
        for b in writes:
            for k, v in b.w.items():
                if need.get(k, 0) < v:
                    need[k] = v
            for k, v in b.r.items():
                if need.get(k, 0) < v:
                    need[k] = v
        seen = self.seen[eng]
        waits = []
        for k, v in need.items():
            if eng == "pe" and k == "pe":
                continue
            if seen.get(k, 0) >= v:
                continue
            seen[k] = v
            waits.append((k, v))
        fn = _snapshot(fn)
        key = sem if sem is not None else eng
        self.cnt[key] = self.cnt.get(key, 0) + inc
        val = self.cnt[key]
        self.q[eng].append((waits, fn, key, inc))
        for b in reads:
            if b.r.get(key, 0) < val:
                b.r[key] = val
        for b in writes:
            b.w = {key: val}
            b.r = {}
        return (key, val)


def build(nlayers=4, nsup=NSUP):
    nc = bass.Bass("TRN2", target_bir_lowering=False)
    xT = nc.dram_tensor("xT", [D, S], F32, kind="ExternalInput").ap()
    vecs_d = nc.dram_tensor("vecs", [128, NV], F32, kind="ExternalInput").ap()
    adaw = nc.dram_tensor("adaw", [4, D, 6 * D], F32, kind="ExternalInput").ap()
    wpack = nc.dram_tensor("wpack", [128, TOTL], F32, kind="ExternalInput").ap()
    outT = nc.dram_tensor("outT", [D, S], F32, kind="ExternalOutput").ap()
    wbf = nc.dram_tensor("wbf", [128, TOTL], BF16).ap()

    P = Prog()
    es = contextlib.ExitStack()
    with es:
        def sb(name, shape, dt):
            return es.enter_context(nc.sbuf_tensor(name, shape, dt))

        xres = sb("xres", [128, 8, NT], F32)
        hbuf = sb("hbuf", [128, 8, NT], BF16)
        hid = sb("hid", [128, FC, NT], BF16)
        v32 = sb("v32", [128, 8, 1040], F32)
        uring = sb("uring", [128, 2, 1056], F32)
        vring = sb("vring", [128, 3, NT], F32)
        wring = sb("wring", [128, NRING, SLOT], BF16)
        sqr = sb("sqr", [128, 2, 512], BF16)
        rstd_t = sb("rstd", [128, 2, 512], F32)
        tmp_t = sb("tmp", [128, 4, 512], F32)
        sg_t = sb("sg", [128, 2, 512], F32)
        rows_t = sb("rows", [1, 2, 512], F32)
        vecs = sb("vecs_sb", [128, NV], F32)
        modT = sb("modT", [128, 4, 48], F32)
        der = sb("der", [128, 4, 32], F32)
        cact = sb("cact", [128, 8], F32)
        haloA = sb("haloA", [128, 2, 8, 2], F32)
        haloB = sb("haloB", [128, 8, 30], F32)
        haloC = sb("haloC", [128, 8, 15], F32)
        ones_bf = sb("ones_bf", [128, 128], BF16)
        ones_f = sb("ones_f", [128, 128], F32)
        one11 = sb("one11", [1, 1], F32)
        small = sb("small", [128, 64], F32)
        psum = es.enter_context(nc.psum_tensor("psum", [128, 8, 512], F32))

        semnames = ["pe", "act", "dve", "pool", "xin", "vec", "ada0", "ada1"]
        semnames += [f"w{k}" for k in range(NRING)]
        semnames += [f"ot{k}" for k in range(4)]
        semnames += [f"cvm{i}" for i in range(4)] + [f"cvf{i}" for i in range(4)]
        sems = {n: es.enter_context(nc.semaphore(n)) for n in semnames}

        xb = [[Buf(f"x{c}_{t}") for t in range(2)] for c in range(8)]
        hb = [[Buf(f"h{c}_{t}") for t in range(2)] for c in range(8)]
        hidb = [[Buf(f"hid{c}_{t}") for t in range(2)] for c in range(FC)]
        vb = [[Buf(f"v{c}_{t}") for t in range(2)] for c in range(8)]
        vhalo = [Buf(f"vh{c}") for c in range(8)]
        ub = [Buf(f"u{k}") for k in range(2)]
        vrb = [Buf(f"vr{k}") for k in range(3)]
        wb = [Buf(f"w{k}") for k in range(NRING)]
        sqb = [Buf(f"sq{k}") for k in range(2)]
        rstdb = [Buf(f"rstd{k}") for k in range(2)]
        tmpb = [Buf(f"tmp{k}") for k in range(4)]
        sgb = [Buf(f"sg{k}") for k in range(2)]
        rowsb = [Buf(f"rows{k}") for k in range(2)]
        bankb = [Buf(f"bank{k}") for k in range(8)]
        adastb = [Buf("adast0"), Buf("adast1")]
        vecsb = Buf("vecs")
        constb = Buf("const")
        modb = [Buf(f"mod{i}") for i in range(4)]
        derb = [Buf(f"der{i}") for i in range(4)]
        cactb = Buf("cact")
        haloAb = [[Buf(f"hA{l}_{c}") for c in range(8)] for l in range(2)]
        haloBb = [Buf(f"hB{c}") for c in range(8)]
        haloCb = [Buf(f"hC{c}") for c in range(8)]
        smallb = Buf("small")
        cvb = [dict(m=Buf(f"cvm{i}"), f=Buf(f"cvf{i}")) for i in range(4)]

        ctr = dict(bank=0, sq=0, rstd=0, tmp=0, sg=0, rows=0, u=0, vr=0, ot=0)

        def nxt(name, n):
            k = ctr[name] % n
            ctr[name] += 1
            return k

        def bank():
            k = nxt("bank", 8)
            return psum[:, k, :], bankb[k]

        def vcol(name, c=0, w=1):
            o = VOFF[name] + c
            return vecs[:, o:o + w]

        P.op("sp", lambda e: e.dma_start(out=vecs[:, :], in_=vecs_d[:, :]), writes=[vecsb], sem="vec", inc=16)
        P.op("pool", lambda e: e.memset(ones_bf[:, :], 1.0 / D), writes=[constb])
        P.op("pool", lambda e: e.memset(ones_f[:, :], 1.0 / D), writes=[constb])
        P.op("pool", lambda e: e.memset(one11[:, :], 1.0), writes=[constb])
        P.op("pool", lambda e: e.memset(small[:, 40:41], EPS), writes=[constb])

        CH = 8192
        for i in range(nlayers):
            for part, key in (("mreg", "m"), ("freg", "f")):
                a, b_ = SLABS[i][part]
                pos = a
                pieces = []
                while pos < b_:
                    e_ = min(pos + CH, b_)
                    pieces.append((pos, e_))
                    pos = e_
                for (p0, p1) in pieces:
                    P.op("pool", lambda e, p0=p0, p1=p1: e.dma_start(out=wbf[:, p0:p1], in_=wpack[:, p0:p1]),
                         sem=f"cv{key}{i}", inc=16)
                cvb[i][key].w = {f"cv{key}{i}": P.cnt[f"cv{key}{i}"]}

        P.op("act", lambda e: e.activation(cact[:, :], vcol("ct", 0, 8), AF.Silu), reads=[vecsb], writes=[cactb])

        def mod_layer(i):
            for es_ in range(12):
                k = (i * 12 + es_) % 2
                stage = v32[:, :, k * 512:(k + 1) * 512]
                src = adaw[i].rearrange("(dc p) e -> p dc e", p=128)[:, :, es_ * 512:(es_ + 1) * 512]
                P.op("sp", lambda e, stage=stage, src=src: e.dma_start(out=stage, in_=src),
                     writes=[adastb[k]], sem=f"ada{k}", inc=16)
                bk, bkb = bank()

                def mv(pe, bk=bk, stage=stage):
                    for dc in range(8):
                        ins = pe.matmul(bk[0:1, :], cact[:, dc:dc + 1], stage[:, dc, :], start=(dc == 0), stop=(dc == 7))
                    return ins

                P.op("pe", mv, reads=[adastb[k], cactb], writes=[bkb])
                rk = nxt("rows", 2)
                P.op("act", lambda e, rk=rk, bk=bk: e.activation(rows_t[0:1, rk, :], bk[0:1, :], AF.Copy),
                     reads=[bkb], writes=[rowsb[rk]])
                bk2, bkb2 = bank()

                def tr(pe, bk2=bk2, rk=rk):
                    for q in range(4):
                        ins = pe.matmul(bk2[:, q:q + 1], rows_t[0:1, rk, q * 128:(q + 1) * 128], one11[0:1, 0:1],
                                        start=True, stop=True)
                    return ins

                P.op("pe", tr, reads=[rowsb[rk], constb], writes=[bkb2])
                P.op("dve", lambda e, bk2=bk2, es_=es_: e.tensor_tensor(
                    modT[:, i, es_ * 4:(es_ + 1) * 4], bk2[:, 0:4], vcol(f"adab{i}", es_ * 4, 4), ALU.add),
                    reads=[bkb2, vecsb], writes=[modb[i]])
            kind = i % 3
            P.op("dve", lambda e: e.scalar_tensor_tensor(der[:, i, 0:8], modT[:, i, 8:16], 1.0, vcol(f"nmg{i}", 0, 8),
                                                          ALU.add, ALU.mult), reads=[modb[i], vecsb], writes=[derb[i]])
            P.op("dve", lambda e: e.scalar_tensor_tensor(der[:, i, 8:16], modT[:, i, 32:40], 1.0, vcol(f"nfg{i}", 0, 8),
                                                          ALU.add, ALU.mult), reads=[modb[i], vecsb], writes=[derb[i]])
            if kind == 2:
                P.op("dve", lambda e: e.tensor_tensor(der[:, i, 16:24], modT[:, i, 16:24], vcol("psc", 0, 8), ALU.mult),
                     reads=[modb[i], vecsb, derb[i]], writes=[derb[i]])
            if kind == 1:
                P.op("dve", lambda e: e.tensor_tensor(der[:, i, 16:24], modT[:, i, 16:24], vcol("bb2", 0, 8), ALU.mult),
                     reads=[modb[i], vecsb, derb[i]], writes=[derb[i]])

        for i in range(nlayers):
            mod_layer(i)

        seq = []
        for s in range(nsup):
            for i in range(nlayers):
                for (o, L) in SLABS[i]["mixer"]:
                    seq.append((i, "m", o, L))
                for (o, L) in SLABS[i]["ffn"]:
                    seq.append((i, "f", o, L))
        wstate = dict(loaded=0, used=0)

        def issue_loads(upto):
            while wstate["loaded"] <= min(upto, len(seq) - 1):
                g = wstate["loaded"]
                i, part, o, L = seq[g]
                k = g % NRING
                P.op("sp", lambda e, k=k, o=o, L=L: e.dma_start(out=wring[:, k, 0:L], in_=wbf[:, o:o + L]),
                     reads=[cvb[i][part]], writes=[wb[k]], sem=f"w{k}", inc=16)
                wstate["loaded"] += 1

        def next_slab():
            g = wstate["used"]
            wstate["used"] += 1
            issue_loads(g + PREFETCH)
            k = g % NRING
            return wring[:, k, :], wb[k]

        def load_x(s):
            for c in range(8):
                P.op("pool", lambda e, c=c: e.dma_start(out=xres[:, c, :], in_=xT[c * 128:(c + 1) * 128, s * NT:(s + 1) * NT]),
                     writes=[xb[c][0], xb[c][1]], sem="xin", inc=16)
            v = P.cnt["xin"]
            for c in range(8):
                for t in range(2):
                    xb[c][t].w = {"xin": v}

        def norm_phase(gs, sh, mode, s=0):
            for nt in range(2):
                sl = slice(nt * 512, (nt + 1) * 512)
                bk, bkb = bank()
                for c in range(8):
                    k = nxt("sq", 2)
                    P.op("act", lambda e, k=k, c=c: e.activation(sqr[:, k, :], xres[:, c, sl], AF.Square),
                         reads=[xb[c][nt]], writes=[sqb[k]])
                    P.op("pe", lambda e, k=k, c=c: e.matmul(bk, ones_bf[:, :], sqr[:, k, :], start=(c == 0), stop=(c == 7)),
                         reads=[sqb[k], constb], writes=[bkb])
                rk = nxt("rstd", 2)
                P.op("act", lambda e, rk=rk: e.activation(rstd_t[:, rk, :], bk, AF.Sqrt, bias=small[:, 40:41], scale=1.0),
                     reads=[bkb, constb], writes=[rstdb[rk]])
                P.op("dve", lambda e, rk=rk: e.reciprocal(rstd_t[:, rk, :], rstd_t[:, rk, :]),
                     reads=[rstdb[rk]], writes=[rstdb[rk]])
                for c in range(8):
                    tk = nxt("tmp", 4)
                    if mode == "out":
                        P.op("dve", lambda e, tk=tk, c=c, rk=rk: e.scalar_tensor_tensor(
                            tmp_t[:, tk, :], xres[:, c, sl], gs[:, c:c + 1], rstd_t[:, rk, :], ALU.mult, ALU.mult),
                            reads=[xb[c][nt], rstdb[rk], vecsb], writes=[tmpb[tk]])
                        P.op("sp", lambda e, tk=tk, c=c: e.dma_start(
                            out=outT[c * 128:(c + 1) * 128, s * NT + nt * 512: s * NT + (nt + 1) * 512], in_=tmp_t[:, tk, :]),
                            reads=[tmpb[tk]], sem=f"ot{tk}", inc=16)
                        continue
                    P.op("dve", lambda e, tk=tk, c=c, rk=rk: e.scalar_tensor_tensor(
                        tmp_t[:, tk, :], xres[:, c, sl], gs[:, c:c + 1], rstd_t[:, rk, :], ALU.mult, ALU.mult),
                        reads=[xb[c][nt], rstdb[rk]] + gs_deps, writes=[tmpb[tk]])
                    if mode == "h":
                        P.op("act", lambda e, tk=tk, c=c: e.activation(hbuf[:, c, sl], tmp_t[:, tk, :], AF.Identity,
                                                                        bias=sh[:, c:c + 1], scale=1.0),
                             reads=[tmpb[tk]] + gs_deps, writes=[hb[c][nt]])
                    else:
                        P.op("act", lambda e, tk=tk, c=c: e.activation(v32[:, c, 15 + nt * 512:15 + (nt + 1) * 512],
                                                                        tmp_t[:, tk, :], AF.Identity,
                                                                        bias=sh[:, c:c + 1], scale=1.0),
                             reads=[tmpb[tk]] + gs_deps, writes=[vb[c][nt]])

        def resid_update(bk, bkb, c, nt, gvec, deps):
            sl = slice(nt * 512, (nt + 1) * 512)
            P.op("dve", lambda e: e.scalar_tensor_tensor(xres[:, c, sl], bk, gvec, xres[:, c, sl], ALU.mult, ALU.add),
                 reads=[bkb] + deps, writes=[xb[c][nt]])

        def proj_group(slab, slabb, col0, kchunks, stride, rhs_t, rhs_bufs, nt):
            sl = slice(nt * 512, (nt + 1) * 512)
            bk, bkb = bank()

            def fn(pe):
                for k in range(kchunks):
                    ins = pe.matmul(bk, slab[:, k * stride + col0:k * stride + col0 + 128], rhs_t[:, k, sl],
                                    start=(k == 0), stop=(k == kchunks - 1))
                return ins

            P.op("pe", fn, reads=[slabb] + [rhs_bufs[k][nt] for k in range(kchunks)], writes=[bkb])
            return bk, bkb

        def mixer_A(i, s):
            la = i // 3
            g1 = modT[:, i, 16:24]
            for j in range(8):
                slab, slabb = next_slab()
                uk = nxt("u", 2)
                U = uring[:, uk, :]
                if s == 0:
                    P.op("pool", lambda e, U=U: e.memset(U[:, 0:2], 0.0), writes=[ub[uk]])
                else:
                    P.op("pool", lambda e, U=U, j=j: e.tensor_copy(U[:, 0:2], haloA[:, la, j, :]),
                         reads=[haloAb[la][j]], writes=[ub[uk]])
                gbs = []
                for nt in range(2):
                    gc, gcb = proj_group(slab, slabb, 0, 8, 384, hbuf, hb, nt)
                    hv, hvb = proj_group(slab, slabb, 128, 8, 384, hbuf, hb, nt)
                    gb, gbb = proj_group(slab, slabb, 256, 8, 384, hbuf, hb, nt)
                    gbs.append((gb, gbb))
                    k = nxt("sg", 2)
                    P.op("act", lambda e, k=k, hv=hv: e.activation(sg_t[:, k, :], hv, AF.Copy), reads=[hvb], writes=[sgb[k]])
                    P.op("dve", lambda e, k=k, gc=gc, nt=nt, U=U: e.tensor_tensor(
                        U[:, 2 + nt * 512:2 + (nt + 1) * 512], gc, sg_t[:, k, :], ALU.mult),
                        reads=[gcb, sgb[k]], writes=[ub[uk]])
                vk = nxt("vr", 3)
                V = vring[:, vk, :]
                w = lambda kk, j=j: vcol(f"acw{la}", kk * 8 + j, 1)
                P.op("dve", lambda e, U=U, V=V, w=w: e.tensor_scalar(V, U[:, 0:NT], w(0), None, ALU.mult),
                     reads=[ub[uk], vecsb], writes=[vrb[vk]])
                P.op("dve", lambda e, U=U, V=V, w=w: e.scalar_tensor_tensor(V, U[:, 1:NT + 1], w(1), V, ALU.mult, ALU.add),
                     reads=[ub[uk], vrb[vk]], writes=[vrb[vk]])
                P.op("dve", lambda e, U=U, V=V, w=w: e.scalar_tensor_tensor(V, U[:, 2:NT + 2], w(2), V, ALU.mult, ALU.add),
                     reads=[ub[uk], vrb[vk]], writes=[vrb[vk]])
                P.op("pool", lambda e, U=U, j=j: e.tensor_copy(haloA[:, la, j, :], U[:, NT:NT + 2]),
                     reads=[ub[uk]], writes=[haloAb[la][j]])
                for nt in range(2):
                    gb, gbb = gbs[nt]
                    P.op("dve", lambda e, gb=gb, nt=nt, V=V, j=j: e.tensor_tensor(
                        hid[:, j, nt * 512:(nt + 1) * 512], gb, V[:, nt * 512:(nt + 1) * 512], ALU.mult),
                        reads=[gbb, vrb[vk]], writes=[hidb[j][nt]])
            for m in range(2):
                slab, slabb = next_slab()
                for q in range(4):
                    c = m * 4 + q
                    for nt in range(2):
                        bk, bkb = proj_group(slab, slabb, q * 128, 8, 512, hid, hidb, nt)
                        resid_update(bk, bkb, c, nt, g1[:, c:c + 1], [modb[i]])

        def mixer_B(i, s):
            g1 = modT[:, i, 16:24]
            for m in range(4):
                slab, slabb = next_slab()
                for q in range(2):
                    j = 2 * m + q
                    uk = nxt("u", 2)
                    U = uring[:, uk, :]
                    if s == 0:
                        P.op("pool", lambda e, U=U: e.memset(U[:, 0:30], 0.0), writes=[ub[uk]])
                    else:
                        P.op("pool", lambda e, U=U, j=j: e.tensor_copy(U[:, 0:30], haloB[:, j, :]),
                             reads=[haloBb[j]], writes=[ub[uk]])
                    for nt in range(2):
                        a, ab = proj_group(slab, slabb, q * 256, 8, 512, hbuf, hb, nt)
                        gt, gtb = proj_group(slab, slabb, q * 256 + 128, 8, 512, hbuf, hb, nt)
                        k = nxt("sg", 2)
                        P.op("act", lambda e, k=k, gt=gt, j=j: e.activation(sg_t[:, k, :], gt, AF.Sigmoid,
                                                                         bias=vcol("bb1", 8 + j, 1), scale=1.0),
                             reads=[gtb, vecsb], writes=[sgb[k]])
                        P.op("dve", lambda e, k=k, a=a, nt=nt, U=U, j=j: e.scalar_tensor_tensor(
                            U[:, 30 + nt * 512:30 + (nt + 1) * 512], a, vcol("bb1", j, 1), sg_t[:, k, :], ALU.add, ALU.mult),
                            reads=[ab, sgb[k], vecsb], writes=[ub[uk]])
                    cw = lambda kk, j=j: vcol("bcw", kk * 8 + j, 1)
                    vj = v32[:, j, 0:NT]
                    vjb = [vb[j][0], vb[j][1]]
                    P.op("dve", lambda e, U=U, vj=vj, cw=cw, j=j: e.tensor_scalar(vj, U[:, 0:NT], cw(0), vcol("bcb", j, 1),
                                                                              ALU.mult, ALU.add),
                         reads=[ub[uk], vecsb], writes=vjb)
                    for kk in range(1, 16):
                        P.op("dve", lambda e, U=U, vj=vj, cw=cw, kk=kk: e.scalar_tensor_tensor(
                            vj, U[:, kk:kk + NT], cw(kk), vj, ALU.mult, ALU.add), reads=[ub[uk]] + vjb, writes=vjb)
                    vk = nxt("vr", 3)
                    V = vring[:, vk, :]
                    P.op("act", lambda e, U=U, V=V, cw=cw: e.activation(V, U[:, 16:16 + NT], AF.Copy, scale=cw(16)),
                         reads=[ub[uk], vecsb], writes=[vrb[vk]])
                    for kk in range(17, 31):
                        tk_ = nxt("vr", 3)
                        if tk_ == vk:
                            tk_ = nxt("vr", 3)
                        T_ = vring[:, tk_, :]
                        P.op("act", lambda e, U=U, T_=T_, cw=cw, kk=kk: e.activation(T_, U[:, kk:kk + NT], AF.Copy, scale=cw(kk)),
                             reads=[ub[uk], vecsb], writes=[vrb[tk_]])
                        P.op("pool", lambda e, V=V, T_=T_: e.tensor_tensor(V, V, T_, ALU.add),
                             reads=[vrb[vk], vrb[tk_]], writes=[vrb[vk]])
                    P.op("pool", lambda e, U=U, j=j: e.tensor_copy(haloB[:, j, :], U[:, NT:NT + 30]),
                         reads=[ub[uk]], writes=[haloBb[j]])
                    P.op("dve", lambda e, V=V, vj=vj: e.tensor_tensor(vj, vj, V, ALU.add), reads=[vrb[vk]] + vjb, writes=vjb)
            for nt in range(2):
                sl = slice(nt * 512, (nt + 1) * 512)
                mu, mub = bank()
                ms, msb = bank()
                for c in range(8):
                    k = nxt("sq", 2)
                    P.op("act", lambda e, k=k, c=c: e.activation(sqr[:, k, :], v32[:, c, sl], AF.Square),
                         reads=[vb[c][nt]], writes=[sqb[k]])
                    P.op("pe", lambda e, k=k, c=c: e.matmul(ms, ones_bf[:, :], sqr[:, k, :], start=(c == 0), stop=(c == 7)),
                         reads=[sqb[k], constb], writes=[msb])
                    P.op("pe", lambda e, c=c: e.matmul(mu, ones_f[:, :], v32[:, c, sl], start=(c == 0), stop=(c == 7)),
                         reads=[vb[c][nt], constb], writes=[mub])
                k = nxt("sg", 2)
                mus = sg_t[:, k, :]
                P.op("act", lambda e, mus=mus: e.activation(mus, mu, AF.Copy), reads=[mub], writes=[sgb[k]])
                rk = nxt("rstd", 2)
                R = rstd_t[:, rk, :]
                P.op("dve", lambda e, mus=mus, R=R: e.tensor_tensor(R, mus, mus, ALU.mult), reads=[sgb[k]], writes=[rstdb[rk]])
                P.op("dve", lambda e, R=R: e.tensor_tensor(R, ms, R, ALU.subtract), reads=[msb, rstdb[rk]], writes=[rstdb[rk]])
                P.op("act", lambda e, R=R: e.activation(R, R, AF.Sqrt, bias=small[:, 40:41], scale=1.0),
                     reads=[rstdb[rk], constb], writes=[rstdb[rk]])
                P.op("dve", lambda e, R=R: e.reciprocal(R, R), reads=[rstdb[rk]], writes=[rstdb[rk]])
                for c in range(8):
                    tk = nxt("tmp", 4)
                    T = tmp_t[:, tk, :]
                    P.op("dve", lambda e, T=T, c=c, mus=mus: e.tensor_tensor(T, v32[:, c, sl], mus, ALU.subtract),
                         reads=[vb[c][nt], sgb[k]], writes=[tmpb[tk]])
                    P.op("pool", lambda e, T=T, R=R: e.tensor_tensor(T, T, R, ALU.mult),
                         reads=[tmpb[tk], rstdb[rk]], writes=[tmpb[tk]])
                    P.op("act", lambda e, T=T, c=c: e.activation(hid[:, c, sl], T, AF.Silu, bias=vcol("blb", c, 1),
                                                                  scale=vcol("blg", c, 1)),
                         reads=[tmpb[tk], vecsb], writes=[hidb[c][nt]])
            for m in range(2):
                slab, slabb = next_slab()
                for q in range(4):
                    c = m * 4 + q
                    for nt in range(2):
                        bk, bkb = proj_group(slab, slabb, q * 128, 8, 512, hid, hidb, nt)
                        resid_update(bk, bkb, c, nt, g1[:, c:c + 1], [modb[i]])
                        sl = slice(nt * 512, (nt + 1) * 512)
                        P.op("pool", lambda e, c=c, sl=sl: e.tensor_scalar(xres[:, c, sl], xres[:, c, sl], der[:, i, 16 + c:17 + c],
                                                                            None, ALU.add),
                             reads=[xb[c][nt], derb[i]], writes=[xb[c][nt]])

        def mixer_C(i, s):
            slab, slabb = next_slab()
            for c in range(8):
                g = c // 2
                w = 2 << g
                H = v32[:, c, :]
                engname = "pool" if c % 2 == 0 else "dve"
                hdeps = [vb[c][0], vb[c][1], vhalo[c]]
                if s == 0:
                    P.op("pool", lambda e, H=H: e.memset(H[:, 0:15], 0.0), writes=[vhalo[c]])
                else:
                    P.op("pool", lambda e, H=H, c=c: e.tensor_copy(H[:, 0:15], haloC[:, c, :]),
                         reads=[haloCb[c]], writes=[vhalo[c]])
                prev, prevb = H, hdeps
                nlev = g + 1
                sbufs = []
                for l in range(1, nlev + 1):
                    a = 15 - (w - (1 << l))
                    sh_ = 1 << (l - 1)
                    uk = nxt("u", 2)
                    Sx = uring[:, uk, :]
                    P.op(engname, lambda e, Sx=Sx, prev=prev, a=a, sh_=sh_: e.tensor_tensor(
                        Sx[:, a:1039], prev[:, a:1039], prev[:, a - sh_:1039 - sh_], ALU.add),
                        reads=prevb, writes=[ub[uk]])
                    prev, prevb = Sx, [ub[uk]]
                P.op("dve", lambda e, prev=prev, H=H, c=c, w=w: e.scalar_tensor_tensor(
                    hbuf[:, c, :], prev[:, 15:1039], 1.0 / w, H[:, 15:1039], ALU.mult, ALU.subtract),
                    reads=prevb + hdeps, writes=[hb[c][0], hb[c][1]])
                if s == 0:
                    P.op(engname, lambda e, prev=prev, g=g: e.tensor_tensor(small[:, 0:16], prev[:, 15:31],
                                                                           vcol("invcnt", g * 16, 16), ALU.mult),
                         reads=prevb + [vecsb, smallb], writes=[smallb])
                    P.op(engname, lambda e, H=H, c=c: e.tensor_tensor(hbuf[:, c, 0:16], small[:, 0:16], H[:, 15:31], ALU.subtract),
                         reads=[smallb] + hdeps, writes=[hb[c][0]])
                P.op("pool", lambda e, H=H, c=c: e.tensor_copy(haloC[:, c, :], H[:, 1024:1039]),
                     reads=hdeps, writes=[haloCb[c]])
            for g in range(4):
                for m in range(2):
                    c = 2 * g + m
                    for nt in range(2):
                        sl = slice(nt * 512, (nt + 1) * 512)
                        bk, bkb = bank()

                        def fn(pe, bk=bk, g=g, m=m, sl=sl):
                            for kc in range(2):
                                o = (g * 2 + kc) * 256 + m * 128
                                ins = pe.matmul(bk, slab[:, o:o + 128], hbuf[:, 2 * g + kc, sl], start=(kc == 0), stop=(kc == 1))
                            return ins

                        P.op("pe", fn, reads=[slabb, hb[2 * g][nt], hb[2 * g + 1][nt]], writes=[bkb])
                        resid_update(bk, bkb, c, nt, der[:, i, 16 + c:17 + c], [derb[i]])

        def ffn(i):
            g2 = modT[:, i, 40:48]
            for m in range(11):
                slab, slabb = next_slab()
                for q in range(2):
                    f = 2 * m + q
                    for nt in range(2):
                        sl = slice(nt * 512, (nt + 1) * 512)
                        gt, gtb = proj_group(slab, slabb, q * 256, 8, 512, hbuf, hb, nt)
                        up, upb = proj_group(slab, slabb, q * 256 + 128, 8, 512, hbuf, hb, nt)
                        k = nxt("sg", 2)
                        P.op("act", lambda e, k=k, gt=gt: e.activation(sg_t[:, k, :], gt, AF.Silu), reads=[gtb], writes=[sgb[k]])
                        P.op("dve", lambda e, k=k, up=up, f=f, sl=sl: e.tensor_tensor(hid[:, f, sl], up, sg_t[:, k, :], ALU.mult),
                             reads=[upb, sgb[k]], writes=[hidb[f][nt]])
            for c in range(8):
                slab, slabb = next_slab()
                for nt in range(2):
                    bk, bkb = proj_group(slab, slabb, 0, FC, 128, hid, hidb, nt)
                    resid_update(bk, bkb, c, nt, g2[:, c:c + 1], [modb[i]])

        gs_deps = []
        for s in range(nsup):
            load_x(s)
            for i in range(nlayers):
                kind = i % 3
                gs_deps = [derb[i], modb[i]]
                norm_phase(der[:, i, 0:8], modT[:, i, 0:8], "h32" if kind == 2 else "h")
                if kind == 0:
                    mixer_A(i, s)
                elif kind == 1:
                    mixer_B(i, s)
                else:
                    mixer_C(i, s)
                norm_phase(der[:, i, 8:16], modT[:, i, 24:32], "h")
                ffn(i)
            norm_phase(vecs[:, VOFF["fg"]:VOFF["fg"] + 8], None, "out", s)
        P.q["sp"].append(([(f"ot{tk}", P.cnt[f"ot{tk}"]) for tk in range(4) if P.cnt.get(f"ot{tk}", 0) > 0], None, None, 0))
        _emit(P, nc, sems)
    return nc


def _emit(P, nc, sems):
    with nc.Block() as block:
        def make(q):
            def body(eng):
                for waits, fn, key, inc in q:
                    for k, v in waits:
                        eng.wait_ge(sems[k], v)
                    if fn is None:
                        continue
                    ins = fn(eng)
                    ins.then_inc(sems[key], inc)
            return body

        block.tensor(make(P.q["pe"]))
        block.scalar(make(P.q["act"]))
        block.vector(make(P.q["dve"]))
        block.gpsimd(make(P.q["pool"]))
        block.sync(make(P.q["sp"]))


_CACHE = {}


def kernel(**inputs):
    inp = {k: np.asarray(v) for k, v in inputs.items()}
    nlayers = NLAYERS
    wpack = _pack_weights(inp)
    adaw = np.ascontiguousarray(inp["ada_w"], dtype=np.float32)
    in_maps = []
    for b in range(8):
        in_maps.append({
            "xT": np.ascontiguousarray(inp["x"][b].T, dtype=np.float32),
            "vecs": _pack_vecs(inp, b),
            "adaw": adaw,
            "wpack": wpack,
        })
    nc = build(nlayers=nlayers)
    res = run_bass_kernel_spmd(nc, in_maps, core_ids=list(range(8)))
    out = np.stack([np.ascontiguousarray(np.asarray(r["outT"]).T) for r in res.results], axis=0)
    return out.astype(np.float32)
```

```python
import contextlib
import types
import numpy as np
import concourse.bass as bass
import concourse.mybir as mybir
from concourse.bass_utils import run_bass_kernel_spmd

F32 = mybir.dt.float32
BF16 = mybir.dt.bfloat16
ALU = mybir.AluOpType
AF = mybir.ActivationFunctionType

D = 1024
S = 4096
NT = 1024
NSUP = S // NT
F = 2816
FC = F // 128
EPS = 1e-6
NRING = 4
SLOT = 4096
PREFETCH = 3
NLAYERS = 4


def _vec_layout():
    off = {}
    n = 0

    def add(name, w=8):
        nonlocal n
        off[name] = n
        n += w

    for i in range(4):
        add(f"nmg{i}")
        add(f"nfg{i}")
        add(f"adab{i}", 48)
    for j in range(2):
        add(f"acw{j}", 24)
    add("bb1", 16)
    add("bcw", 248)
    add("bcb")
    add("blg")
    add("blb")
    add("bb2")
    add("psc")
    add("fg")
    add("invcnt", 64)
    add("ct", 8)
    return off, n


VOFF, NV = _vec_layout()


def _layer_slabs(i):
    kind = i % 3
    if kind == 0:
        mixer = [("ain", 3072)] * 8 + [("aout", 4096)] * 2
    elif kind == 1:
        mixer = [("bpw1", 4096)] * 4 + [("bpw2", 4096)] * 2
    else:
        mixer = [("cgrp", 2048)]
    ffn = [("fin", 4096)] * 11 + [("fout", 2816)] * 8
    return mixer, ffn


def _slab_table():
    tab = []
    off = 0
    for i in range(4):
        mixer, ffn = _layer_slabs(i)
        m0 = off
        ml = []
        for _, L in mixer:
            ml.append((off, L))
            off += L
        f0 = off
        fl = []
        for _, L in ffn:
            fl.append((off, L))
            off += L
        tab.append(dict(mixer=ml, ffn=fl, mreg=(m0, f0), freg=(f0, off)))
    return tab, off


SLABS, TOTL = _slab_table()


def _colvec(v):
    v = np.asarray(v, np.float32).reshape(-1, 128)
    return np.ascontiguousarray(v.T)


def _pack_cols(w, cols):
    K = w.shape[0]
    sub = w[:, cols]
    n = sub.shape[1]
    return sub.reshape(K // 128, 128, n).transpose(1, 0, 2).reshape(128, (K // 128) * n)


def _pack_weights(inp):
    ar = np.arange
    parts = []
    for i in range(4):
        kind, j = i % 3, i // 3
        if kind == 0:
            w = inp["a_w_in"][j]
            for c in range(8):
                cols = np.concatenate([1024 + c * 128 + ar(128), 2048 + c * 128 + ar(128), c * 128 + ar(128)])
                parts.append(_pack_cols(w, cols))
            w = inp["a_w_out"][j]
            for m in range(2):
                parts.append(_pack_cols(w, m * 512 + ar(512)))
        elif kind == 1:
            w = inp["b_w_pw1"][j]
            for m in range(4):
                cols = np.concatenate([(2 * m) * 128 + ar(128), 1024 + (2 * m) * 128 + ar(128),
                                       (2 * m + 1) * 128 + ar(128), 1024 + (2 * m + 1) * 128 + ar(128)])
                parts.append(_pack_cols(w, cols))
            w = inp["b_w_pw2"][j]
            for m in range(2):
                parts.append(_pack_cols(w, m * 512 + ar(512)))
        else:
            w = inp["p_w_grp"][j]
            parts.append(np.concatenate([_pack_cols(w[g], ar(256)) for g in range(4)], axis=1))
        w = inp["ffn_w_in"][i]
        for m in range(11):
            cols = np.concatenate([(2 * m) * 128 + ar(128), F + (2 * m) * 128 + ar(128),
                                   (2 * m + 1) * 128 + ar(128), F + (2 * m + 1) * 128 + ar(128)])
            parts.append(_pack_cols(w, cols))
        w = inp["ffn_w_out"][i]
        for c in range(8):
            parts.append(_pack_cols(w, c * 128 + ar(128)))
    out = np.ascontiguousarray(np.concatenate(parts, axis=1), dtype=np.float32)
    assert out.shape == (128, TOTL), out.shape
    return out


def _pack_vecs(inp, b):
    v = np.zeros((128, NV), np.float32)

    def put(name, arr):
        arr = np.asarray(arr, np.float32)
        v[:, VOFF[name]:VOFF[name] + arr.shape[1]] = arr

    for i in range(4):
        put(f"nmg{i}", _colvec(inp["norm_mix_g"][i]))
        put(f"nfg{i}", _colvec(inp["norm_ffn_g"][i]))
        put(f"adab{i}", _colvec(inp["ada_b"][i]))
    for j in range(2):
        put(f"acw{j}", np.concatenate([_colvec(inp["a_conv_w"][j][k]) for k in range(3)], axis=1))
    put("bb1", _colvec(inp["b_b_pw1"][0]))
    put("bcw", np.concatenate([_colvec(inp["b_conv_w"][0][k]) for k in range(31)], axis=1))
    put("bcb", _colvec(inp["b_conv_b"][0]))
    put("blg", _colvec(inp["b_ln_g"][0]))
    put("blb", _colvec(inp["b_ln_b"][0]))
    put("bb2", _colvec(inp["b_b_pw2"][0]))
    put("psc", _colvec(inp["p_scale"][0]))
    put("fg", _colvec(inp["final_g"]))
    ic = np.zeros((4, 16), np.float32)
    for g, w in enumerate((2, 4, 8, 16)):
        ic[g] = 1.0 / np.minimum(np.arange(16) + 1, w)
    put("invcnt", np.broadcast_to(ic.reshape(1, 64), (128, 64)))
    put("ct", _colvec(inp["c"][b]))
    return v


class Buf:
    __slots__ = ("name", "w", "r")

    def __init__(self, name):
        self.name = name
        self.w = {}
        self.r = {}


def _snapshot(fn):
    if fn is None or fn.__closure__ is None:
        return fn
    cells = []
    for c in fn.__closure__:
        try:
            v = c.cell_contents
            if isinstance(v, types.FunctionType):
                v = _snapshot(v)
            cells.append(types.CellType(v))
        except ValueError:
            cells.append(c)
    return types.FunctionType(fn.__code__, fn.__globals__, fn.__name__, fn.__defaults__, tuple(cells))


class Prog:
    ENGS = ("pe", "act", "dve", "pool", "sp")

    def __init__(self):
        self.q = {e: [] for e in self.ENGS}
        self.cnt = {}
        self.seen = {e: {} for e in self.ENGS}

    def op(self, eng, fn, reads=(), writes=(), sem=None, inc=1):
        need = {}
        for b in reads:
            for k, v in b.w.items():
                if need.get(k, 0) < v:
                    need[k] = v
        for b in writes:
            for k, v in b.w.items():
                if need.get(k, 0) < v:
                    need[k] = v
            for k, v in b.r.items():
                if need.get(k, 0) < v:
                    need[k] = v
        seen = self.seen[eng]
        waits = []
        for k, v in need.items():
            if eng == "pe" and k == "pe":
                continue
            if seen.get(k, 0) >= v:
                continue
            seen[k] = v
            waits.append((k, v))
        fn = _snapshot(fn)
        key = sem if sem is not None else eng
        self.cnt[key] = self.cnt.get(key, 0) + inc
        val = self.cnt[key]
        self.q[eng].append((waits, fn, key, inc))
        for b in reads:
            if b.r.get(key, 0) < val:
                b.r[key] = val
        for b in writes:
            b.w = {key: val}
            b.r = {}
        return (key, val)


ADA_OFF = TOTL
WBF_L = TOTL + 48 * 4096


def _slab_order(nlayers, nsup):
    seq = []
    for s in range(nsup):
        for i in range(nlayers):
            if s == 0 and i == 0:
                for e_ in range(12):
                    seq.append((0, "a", ADA_OFF + e_ * 4096, 4096))
            for (o, L) in SLABS[i]["mixer"]:
                seq.append((i, "m", o, L))
            domod = (s == 0 and i + 1 < nlayers)
            ffn = SLABS[i]["ffn"]
            for m in range(11):
                seq.append((i, "f") + ffn[m])
                if domod:
                    seq.append((i + 1, "a", ADA_OFF + ((i + 1) * 12 + m) * 4096, 4096))
            for c in range(8):
                seq.append((i, "f") + ffn[11 + c])
                if domod and c == 0:
                    seq.append((i + 1, "a", ADA_OFF + ((i + 1) * 12 + 11) * 4096, 4096))
    return seq


def build(nlayers=4, nsup=NSUP):
    nc = bass.Bass("TRN2", target_bir_lowering=False)
    xT = nc.dram_tensor("xT", [D, S], F32, kind="ExternalInput").ap()
    vecs_d = nc.dram_tensor("vecs", [128, NV], F32, kind="ExternalInput").ap()
    adaw = nc.dram_tensor("adaw", [4, D, 6 * D], F32, kind="ExternalInput").ap()
    wpack = nc.dram_tensor("wpack", [128, TOTL], F32, kind="ExternalInput").ap()
    ident_d = nc.dram_tensor("ident", [128, 128], F32, kind="ExternalInput").ap()
    outT = nc.dram_tensor("outT", [D, S], F32, kind="ExternalOutput").ap()
    wbf = nc.dram_tensor("wbf", [128, WBF_L], BF16).ap()

    P = Prog()
    es = contextlib.ExitStack()
    with es:
        def sb(name, shape, dt):
            return es.enter_context(nc.sbuf_tensor(name, shape, dt))

        xres = sb("xres", [128, 8, NT], F32)
        hbuf = sb("hbuf", [128, 8, NT], BF16)
        hid2 = sb("hid", [128, FC * NT], BF16)
        v32 = sb("v32", [128, 8, 1040], F32)
        ubf = sb("ubf", [128, 2, 1056], BF16)
        dg = sb("dg", [128, 2, 31, 128], BF16)
        wring = sb("wring", [128, NRING, SLOT], BF16)
        sqr = sb("sqr", [128, 2, 512], BF16)
        rstd_t = sb("rstd", [128, 2, 512], F32)
        tmp_t = sb("tmp", [128, 4, 512], F32)
        sg_t = sb("sg", [128, 2, 512], F32)
        vecs = sb("vecs_sb", [128, NV], F32)
        modT = sb("modT", [128, 4, 48], F32)
        der = sb("der", [128, 4, 32], F32)
        cact = sb("cact", [128, 8], BF16)
        haloA = sb("haloA", [128, 2, 8, 2], F32)
        haloB = sb("haloB", [128, 8, 30], BF16)
        haloC = sb("haloC", [128, 8, 15], F32)
        ones_bf = sb("ones_bf", [128, 128], BF16)
        ones_f = sb("ones_f", [128, 128], F32)
        ident_bf = sb("ident_bf", [128, 128], BF16)
        one11 = sb("one11", [1, 1], F32)
        small = sb("small", [128, 64], F32)
        psum = es.enter_context(nc.psum_tensor("psum", [128, 8, 512], F32))

        def hid(f, sl):
            return hid2[:, f * NT + sl.start:f * NT + sl.stop]

        semnames = ["pe", "act", "dve", "pool", "xin", "vec", "idn"]
        semnames += [f"w{k}" for k in range(NRING)]
        semnames += [f"ot{k}" for k in range(4)]
        semnames += [f"cv{x}{i}" for i in range(4) for x in "amf"]
        sems = {n: es.enter_context(nc.semaphore(n)) for n in semnames}

        xb = [[Buf(f"x{c}_{t}") for t in range(2)] for c in range(8)]
        hb = [[Buf(f"h{c}_{t}") for t in range(2)] for c in range(8)]
        hidb = [[Buf(f"hid{c}_{t}") for t in range(2)] for c in range(FC)]
        vb = [[Buf(f"v{c}_{t}") for t in range(2)] for c in range(8)]
        vhalo = [Buf(f"vh{c}") for c in range(8)]
        ubfb = [Buf(f"ubf{k}") for k in range(2)]
        dgb = [Buf(f"dg{k}") for k in range(2)]
        wb = [Buf(f"w{k}") for k in range(NRING)]
        sqb = [Buf(f"sq{k}") for k in range(2)]
        rstdb = [Buf(f"rstd{k}") for k in range(2)]
        tmpb = [Buf(f"tmp{k}") for k in range(4)]
        sgb = [Buf(f"sg{k}") for k in range(2)]
        bankb = [Buf(f"bank{k}") for k in range(8)]
        vecsb = Buf("vecs")
        constb = Buf("const")
        modb = [Buf(f"mod{i}") for i in range(4)]
        derb = [Buf(f"der{i}") for i in range(4)]
        cactb = Buf("cact")
        haloAb = [[Buf(f"hA{l}_{c}") for c in range(8)] for l in range(2)]
        haloBb = [Buf(f"hB{c}") for c in range(8)]
        haloCb = [Buf(f"hC{c}") for c in range(8)]
        smallb = Buf("small")
        cvb = [dict(a=Buf(f"cva{i}"), m=Buf(f"cvm{i}"), f=Buf(f"cvf{i}")) for i in range(4)]

        ctr = dict(bank=0, sq=0, rstd=0, tmp=0, sg=0, vt=0, st=0, ubf=0, dg=0)

        def nxt(name, n):
            k = ctr[name] % n
            ctr[name] += 1
            return k

        def bank():
            k = nxt("bank", 8)
            return psum[:, k, :], bankb[k]

        def vcol(name, c=0, w=1):
            o = VOFF[name] + c
            return vecs[:, o:o + w]

        def vtile():
            c = nxt("vt", 8)
            return v32[:, c, :], [vb[c][0], vb[c][1], vhalo[c]]

        def stile():
            k = nxt("st", 4)
            ap = hid2[:, 3 * k * NT:3 * k * NT + 2080].bitcast(F32)
            return ap, [hidb[3 * k + q][t] for q in range(3) for t in range(2)]

        P.op("sp", lambda e: e.dma_start(out=vecs[:, :], in_=vecs_d[:, :]), writes=[vecsb], sem="vec", inc=16)
        P.op("pool", lambda e: e.memset(ones_bf[:, :], 1.0 / D), writes=[constb])
        P.op("pool", lambda e: e.memset(ones_f[:, :], 1.0 / D), writes=[constb])
        P.op("pool", lambda e: e.memset(one11[:, :], 1.0), writes=[constb])
        P.op("pool", lambda e: e.memset(small[:, 40:41], EPS), writes=[constb])

        def load_x(s):
            for c in range(8):
                P.op("pool", lambda e, c=c: e.dma_start(out=xres[:, c, :], in_=xT[c * 128:(c + 1) * 128, s * NT:(s + 1) * NT]),
                     writes=[xb[c][0], xb[c][1]], sem="xin", inc=16)
            v = P.cnt["xin"]
            for c in range(8):
                for t in range(2):
                    xb[c][t].w = {"xin": v}

        CH = 8192

        def convert_region(i, key):
            if key == "a":
                for e_ in range(12):
                    o = ADA_OFF + (i * 12 + e_) * 4096
                    src_ap = adaw[i].rearrange("(dc p) e -> p dc e", p=128)[:, :, e_ * 512:(e_ + 1) * 512]
                    dst_ap = wbf[:, o:o + 4096].rearrange("p (dc e) -> p dc e", dc=8)
                    P.op("pool", lambda e, s_=src_ap, d_=dst_ap: e.dma_start(out=d_, in_=s_), sem=f"cva{i}", inc=16)
            else:
                a_, b_ = SLABS[i]["mreg" if key == "m" else "freg"]
                pos = a_
                while pos < b_:
                    e2 = min(pos + CH, b_)
                    P.op("pool", lambda e, p0=pos, p1=e2: e.dma_start(out=wbf[:, p0:p1], in_=wpack[:, p0:p1]),
                         sem=f"cv{key}{i}", inc=16)
                    pos = e2
            cvb[i][key].w = {f"cv{key}{i}": P.cnt[f"cv{key}{i}"]}

        P.op("pool", lambda e: e.dma_start(out=ident_bf[:, :], in_=ident_d[:, :]), writes=[constb], sem="idn", inc=16)
        constb.w = {"idn": P.cnt["idn"], "pool": P.cnt["pool"]}
        convert_region(0, "a")
        convert_region(0, "m")
        load_x(0)
        convert_region(0, "f")
        for i in range(1, nlayers):
            for key in "amf":
                convert_region(i, key)

        P.op("act", lambda e: e.activation(cact[:, :], vcol("ct", 0, 8), AF.Silu), reads=[vecsb], writes=[cactb])

        seq = _slab_order(nlayers, nsup)
        wstate = dict(loaded=0, used=0)

        def issue_loads(upto):
            while wstate["loaded"] <= min(upto, len(seq) - 1):
                g = wstate["loaded"]
                i, part, o, L = seq[g]
                k = g % NRING
                P.op("sp", lambda e, k=k, o=o, L=L: e.dma_start(out=wring[:, k, 0:L], in_=wbf[:, o:o + L]),
                     reads=[cvb[i][part]], writes=[wb[k]], sem=f"w{k}", inc=16)
                wstate["loaded"] += 1

        def next_slab(expect):
            g = wstate["used"]
            assert seq[g][0:2] == expect, (g, seq[g], expect)
            wstate["used"] += 1
            issue_loads(g + PREFETCH)
            k = g % NRING
            return wring[:, k, :], wb[k]

        def mod_A(i, e_):
            slab, slabb = next_slab((i, "a"))
            bk, bkb = bank()

            def mv(pe):
                for dc in range(8):
                    ins = pe.matmul(bk[0:1, :], cact[:, dc:dc + 1], slab[:, dc * 512:(dc + 1) * 512], start=(dc == 0), stop=(dc == 7))
                return ins

            P.op("pe", mv, reads=[slabb, cactb], writes=[bkb])
            tk = nxt("tmp", 4)
            P.op("act", lambda e: e.activation(tmp_t[0:1, tk, :], bk[0:1, :], AF.Copy), reads=[bkb], writes=[tmpb[tk]])
            return (e_, tk)

        def mod_B(i, e_, tk):
            bk2, bkb2 = bank()

            def tr(pe):
                for q in range(4):
                    ins = pe.matmul(bk2[:, q:q + 1], tmp_t[0:1, tk, q * 128:(q + 1) * 128], one11[0:1, 0:1], start=True, stop=True)
                return ins

            P.op("pe", tr, reads=[tmpb[tk], constb], writes=[bkb2])
            P.op("dve", lambda e: e.tensor_tensor(modT[:, i, e_ * 4:(e_ + 1) * 4], bk2[:, 0:4], vcol(f"adab{i}", e_ * 4, 4), ALU.add),
                 reads=[bkb2, vecsb], writes=[modb[i]])

        def mod_finish(i):
            kind = i % 3
            P.op("dve", lambda e: e.scalar_tensor_tensor(der[:, i, 0:8], modT[:, i, 8:16], 1.0, vcol(f"nmg{i}", 0, 8),
                                                          ALU.add, ALU.mult), reads=[modb[i], vecsb], writes=[derb[i]])
            P.op("dve", lambda e: e.scalar_tensor_tensor(der[:, i, 8:16], modT[:, i, 32:40], 1.0, vcol(f"nfg{i}", 0, 8),
                                                          ALU.add, ALU.mult), reads=[modb[i], vecsb], writes=[derb[i]])
            if kind == 2:
                P.op("dve", lambda e: e.tensor_tensor(der[:, i, 16:24], modT[:, i, 16:24], vcol("psc", 0, 8), ALU.mult),
                     reads=[modb[i], vecsb, derb[i]], writes=[derb[i]])
            if kind == 1:
                P.op("dve", lambda e: e.tensor_tensor(der[:, i, 16:24], modT[:, i, 16:24], vcol("bb2", 0, 8), ALU.mult),
                     reads=[modb[i], vecsb, derb[i]], writes=[derb[i]])

        for e_ in range(12):
            mod_B(0, *mod_A(0, e_))
        mod_finish(0)

        def norm_phase(gs, sh, mode, deps, s=0):
            for nt in range(2):
                sl = slice(nt * 512, (nt + 1) * 512)
                bk, bkb = bank()
                for c in range(8):
                    k = nxt("sq", 2)
                    P.op("act", lambda e: e.activation(sqr[:, k, :], xres[:, c, sl], AF.Square),
                         reads=[xb[c][nt]], writes=[sqb[k]])
                    P.op("pe", lambda e: e.matmul(bk, ones_bf[:, :], sqr[:, k, :], start=(c == 0), stop=(c == 7)),
                         reads=[sqb[k], constb], writes=[bkb])
                rk = nxt("rstd", 2)
                P.op("act", lambda e: e.activation(rstd_t[:, rk, :], bk, AF.Sqrt, bias=small[:, 40:41], scale=1.0),
                     reads=[bkb, constb], writes=[rstdb[rk]])
                P.op("dve", lambda e: e.reciprocal(rstd_t[:, rk, :], rstd_t[:, rk, :]),
                     reads=[rstdb[rk]], writes=[rstdb[rk]])
                for c in range(8):
                    tk = nxt("tmp", 4)
                    P.op("dve", lambda e: e.scalar_tensor_tensor(
                        tmp_t[:, tk, :], xres[:, c, sl], gs[:, c:c + 1], rstd_t[:, rk, :], ALU.mult, ALU.mult),
                        reads=[xb[c][nt], rstdb[rk]] + deps, writes=[tmpb[tk]])
                    if mode == "out":
                        P.op("sp", lambda e: e.dma_start(
                            out=outT[c * 128:(c + 1) * 128, s * NT + nt * 512: s * NT + (nt + 1) * 512], in_=tmp_t[:, tk, :]),
                            reads=[tmpb[tk]], sem=f"ot{tk}", inc=16)
                    elif mode == "h":
                        P.op("act", lambda e: e.activation(hbuf[:, c, sl], tmp_t[:, tk, :], AF.Identity,
                                                            bias=sh[:, c:c + 1], scale=1.0),
                             reads=[tmpb[tk]] + deps, writes=[hb[c][nt]])
                    else:
                        P.op("act", lambda e: e.activation(v32[:, c, 15 + nt * 512:15 + (nt + 1) * 512],
                                                            tmp_t[:, tk, :], AF.Identity, bias=sh[:, c:c + 1], scale=1.0),
                             reads=[tmpb[tk]] + deps, writes=[vb[c][nt]])

        def resid_update(bk, bkb, c, nt, gvec, deps):
            sl = slice(nt * 512, (nt + 1) * 512)
            P.op("dve", lambda e: e.scalar_tensor_tensor(xres[:, c, sl], bk, gvec, xres[:, c, sl], ALU.mult, ALU.add),
                 reads=[bkb] + deps, writes=[xb[c][nt]])

        def proj_group(slab, slabb, col0, kchunks, stride, rhs_fn, rhs_bufs, nt):
            sl = slice(nt * 512, (nt + 1) * 512)
            bk, bkb = bank()

            def fn(pe):
                for k in range(kchunks):
                    ins = pe.matmul(bk, slab[:, k * stride + col0:k * stride + col0 + 128], rhs_fn(k, sl),
                                    start=(k == 0), stop=(k == kchunks - 1))
                return ins

            P.op("pe", fn, reads=[slabb] + [rhs_bufs[k][nt] for k in range(kchunks)], writes=[bkb])
            return bk, bkb

        h_rhs = lambda k, sl: hbuf[:, k, sl]
        hid_rhs = lambda k, sl: hid(k, sl)

        def mixer_A(i, s):
            la = i // 3
            g1 = modT[:, i, 16:24]
            for j in range(8):
                slab, slabb = next_slab((i, "m"))
                U, Ub = vtile()
                if s == 0:
                    P.op("pool", lambda e: e.memset(U[:, 0:2], 0.0), writes=Ub)
                else:
                    P.op("pool", lambda e: e.tensor_copy(U[:, 0:2], haloA[:, la, j, :]), reads=[haloAb[la][j]], writes=Ub)
                gbs = []
                for nt in range(2):
                    gc, gcb = proj_group(slab, slabb, 0, 8, 384, h_rhs, hb, nt)
                    hv, hvb = proj_group(slab, slabb, 128, 8, 384, h_rhs, hb, nt)
                    gb, gbb = proj_group(slab, slabb, 256, 8, 384, h_rhs, hb, nt)
                    k = nxt("sg", 2)
                    P.op("act", lambda e: e.activation(sg_t[:, k, :], hv, AF.Copy), reads=[hvb], writes=[sgb[k]])
                    P.op("dve", lambda e: e.tensor_tensor(U[:, 2 + nt * 512:2 + (nt + 1) * 512], gc, sg_t[:, k, :], ALU.mult),
                         reads=[gcb, sgb[k]], writes=Ub)
                    tk = nxt("tmp", 4)
                    P.op("act", lambda e: e.activation(tmp_t[:, tk, :], gb, AF.Copy), reads=[gbb], writes=[tmpb[tk]])
                    gbs.append(tk)
                V, Vb = vtile()
                w = lambda kk: vcol(f"acw{la}", kk * 8 + j, 1)
                P.op("dve", lambda e: e.tensor_scalar(V[:, 0:NT], U[:, 0:NT], w(0), None, ALU.mult),
                     reads=Ub + [vecsb], writes=Vb)
                P.op("dve", lambda e: e.scalar_tensor_tensor(V[:, 0:NT], U[:, 1:NT + 1], w(1), V[:, 0:NT], ALU.mult, ALU.add),
                     reads=Ub + Vb, writes=Vb)
                P.op("dve", lambda e: e.scalar_tensor_tensor(V[:, 0:NT], U[:, 2:NT + 2], w(2), V[:, 0:NT], ALU.mult, ALU.add),
                     reads=Ub + Vb, writes=Vb)
                P.op("pool", lambda e: e.tensor_copy(haloA[:, la, j, :], U[:, NT:NT + 2]), reads=Ub, writes=[haloAb[la][j]])
                for nt in range(2):
                    tk = gbs[nt]
                    sl = slice(nt * 512, (nt + 1) * 512)
                    P.op("dve", lambda e: e.tensor_tensor(hid(j, sl), tmp_t[:, tk, :], V[:, sl], ALU.mult),
                         reads=[tmpb[tk]] + Vb, writes=[hidb[j][nt]])
            for m in range(2):
                slab, slabb = next_slab((i, "m"))
                for q in range(4):
                    c = m * 4 + q
                    for nt in range(2):
                        bk, bkb = proj_group(slab, slabb, q * 128, 8, 512, hid_rhs, hidb, nt)
                        resid_update(bk, bkb, c, nt, g1[:, c:c + 1], [modb[i]])

        def mixer_B(i, s):
            g1 = modT[:, i, 16:24]
            for m in range(4):
                slab, slabb = next_slab((i, "m"))
                for q in range(2):
                    j = 2 * m + q
                    uk = nxt("ubf", 2)
                    U = ubf[:, uk, :]
                    if s == 0:
                        P.op("pool", lambda e: e.memset(U[:, 0:30], 0.0), writes=[ubfb[uk]])
                    else:
                        P.op("pool", lambda e: e.tensor_copy(U[:, 0:30], haloB[:, j, :]), reads=[haloBb[j]], writes=[ubfb[uk]])
                    dk = nxt("dg", 2)

                    def mkdiag(e):
                        for kk in range(31):
                            ins = e.tensor_scalar(dg[:, dk, kk, :], ident_bf[:, :], vcol("bcw", kk * 8 + j, 1), None, ALU.mult)
                        return ins

                    P.op("dve", mkdiag, reads=[vecsb, constb], writes=[dgb[dk]])
                    for nt in range(2):
                        a, ab = proj_group(slab, slabb, q * 256, 8, 512, h_rhs, hb, nt)
                        gt, gtb = proj_group(slab, slabb, q * 256 + 128, 8, 512, h_rhs, hb, nt)
                        k = nxt("sg", 2)
                        P.op("act", lambda e: e.activation(sg_t[:, k, :], gt, AF.Sigmoid, bias=vcol("bb1", 8 + j, 1), scale=1.0),
                             reads=[gtb, vecsb], writes=[sgb[k]])
                        P.op("dve", lambda e: e.scalar_tensor_tensor(
                            U[:, 30 + nt * 512:30 + (nt + 1) * 512], a, vcol("bb1", j, 1), sg_t[:, k, :], ALU.add, ALU.mult),
                            reads=[ab, sgb[k], vecsb], writes=[ubfb[uk]])
                    P.op("pool", lambda e: e.tensor_copy(haloB[:, j, :], U[:, NT:NT + 30]), reads=[ubfb[uk]], writes=[haloBb[j]])
                    for nt in range(2):
                        bk, bkb = bank()

                        def conv(pe):
                            for kk in range(31):
                                ins = pe.matmul(bk, dg[:, dk, kk, :], U[:, kk + nt * 512:kk + nt * 512 + 512],
                                                start=(kk == 0), stop=(kk == 30))
                            return ins

                        P.op("pe", conv, reads=[dgb[dk], ubfb[uk]], writes=[bkb])
                        P.op("act", lambda e: e.activation(v32[:, j, nt * 512:(nt + 1) * 512], bk, AF.Identity,
                                                            bias=vcol("bcb", j, 1), scale=1.0),
                             reads=[bkb, vecsb], writes=[vb[j][nt]])
            for nt in range(2):
                sl = slice(nt * 512, (nt + 1) * 512)
                mu, mub = bank()
                ms, msb = bank()
                for c in range(8):
                    k = nxt("sq", 2)
                    P.op("act", lambda e: e.activation(sqr[:, k, :], v32[:, c, sl], AF.Square), reads=[vb[c][nt]], writes=[sqb[k]])
                    P.op("pe", lambda e: e.matmul(ms, ones_bf[:, :], sqr[:, k, :], start=(c == 0), stop=(c == 7)),
                         reads=[sqb[k], constb], writes=[msb])
                    P.op("pe", lambda e: e.matmul(mu, ones_f[:, :], v32[:, c, sl], start=(c == 0), stop=(c == 7)),
                         reads=[vb[c][nt], constb], writes=[mub])
                k = nxt("sg", 2)
                mus = sg_t[:, k, :]
                P.op("act", lambda e: e.activation(mus, mu, AF.Copy), reads=[mub], writes=[sgb[k]])
                rk = nxt("rstd", 2)
                R = rstd_t[:, rk, :]
                P.op("dve", lambda e: e.tensor_tensor(R, mus, mus, ALU.mult), reads=[sgb[k]], writes=[rstdb[rk]])
                P.op("dve", lambda e: e.tensor_tensor(R, ms, R, ALU.subtract), reads=[msb, rstdb[rk]], writes=[rstdb[rk]])
                P.op("act", lambda e: e.activation(R, R, AF.Sqrt, bias=small[:, 40:41], scale=1.0),
                     reads=[rstdb[rk], constb], writes=[rstdb[rk]])
                P.op("dve", lambda e: e.reciprocal(R, R), reads=[rstdb[rk]], writes=[rstdb[rk]])
                for c in range(8):
                    tk = nxt("tmp", 4)
                    T = tmp_t[:, tk, :]
                    P.op("dve", lambda e: e.tensor_tensor(T, v32[:, c, sl], mus, ALU.subtract),
                         reads=[vb[c][nt], sgb[k]], writes=[tmpb[tk]])
                    P.op("dve", lambda e: e.tensor_tensor(T, T, R, ALU.mult), reads=[tmpb[tk], rstdb[rk]], writes=[tmpb[tk]])
                    P.op("act", lambda e: e.activation(hid(c, sl), T, AF.Silu, bias=vcol("blb", c, 1), scale=vcol("blg", c, 1)),
                         reads=[tmpb[tk], vecsb], writes=[hidb[c][nt]])
            for m in range(2):
                slab, slabb = next_slab((i, "m"))
                for q in range(4):
                    c = m * 4 + q
                    for nt in range(2):
                        bk, bkb = proj_group(slab, slabb, q * 128, 8, 512, hid_rhs, hidb, nt)
                        resid_update(bk, bkb, c, nt, g1[:, c:c + 1], [modb[i]])
                        sl = slice(nt * 512, (nt + 1) * 512)
                        P.op("pool", lambda e: e.tensor_scalar(xres[:, c, sl], xres[:, c, sl], der[:, i, 16 + c:17 + c], None, ALU.add),
                             reads=[xb[c][nt], derb[i]], writes=[xb[c][nt]])

        def mixer_C(i, s):
            slab, slabb = next_slab((i, "m"))
            for c in range(8):
                g = c // 2
                w = 2 << g
                H = v32[:, c, :]
                engname = "pool" if c % 2 == 0 else "dve"
                hdeps = [vb[c][0], vb[c][1], vhalo[c]]
                if s == 0:
                    P.op("pool", lambda e: e.memset(H[:, 0:15], 0.0), writes=[vhalo[c]])
                else:
                    P.op("pool", lambda e: e.tensor_copy(H[:, 0:15], haloC[:, c, :]), reads=[haloCb[c]], writes=[vhalo[c]])
                prev, prevb = H, hdeps
                for l in range(1, g + 2):
                    a = 15 - (w - (1 << l))
                    sh_ = 1 << (l - 1)
                    Sx, Sb = stile()
                    P.op(engname, lambda e: e.tensor_tensor(Sx[:, a:1039], prev[:, a:1039], prev[:, a - sh_:1039 - sh_], ALU.add),
                         reads=prevb, writes=Sb)
                    prev, prevb = Sx, Sb
                P.op("dve", lambda e: e.scalar_tensor_tensor(hbuf[:, c, :], prev[:, 15:1039], 1.0 / w, H[:, 15:1039],
                                                              ALU.mult, ALU.subtract),
                     reads=prevb + hdeps, writes=[hb[c][0], hb[c][1]])
                if s == 0:
                    P.op("dve", lambda e: e.tensor_tensor(small[:, 0:16], prev[:, 15:31], vcol("invcnt", g * 16, 16), ALU.mult),
                         reads=prevb + [vecsb, smallb], writes=[smallb])
                    P.op("dve", lambda e: e.tensor_tensor(hbuf[:, c, 0:16], small[:, 0:16], H[:, 15:31], ALU.subtract),
                         reads=[smallb] + hdeps, writes=[hb[c][0]])
                P.op("pool", lambda e: e.tensor_copy(haloC[:, c, :], H[:, 1024:1039]), reads=hdeps, writes=[haloCb[c]])
            for g in range(4):
                for m in range(2):
                    c = 2 * g + m
                    for nt in range(2):
                        sl = slice(nt * 512, (nt + 1) * 512)
                        bk, bkb = bank()

                        def fn(pe):
                            for kc in range(2):
                                o = (g * 2 + kc) * 256 + m * 128
                                ins = pe.matmul(bk, slab[:, o:o + 128], hbuf[:, 2 * g + kc, sl], start=(kc == 0), stop=(kc == 1))
                            return ins

                        P.op("pe", fn, reads=[slabb, hb[2 * g][nt], hb[2 * g + 1][nt]], writes=[bkb])
                        resid_update(bk, bkb, c, nt, der[:, i, 16 + c:17 + c], [derb[i]])

        def ffn(i, s):
            g2 = modT[:, i, 40:48]
            domod = (s == 0 and i + 1 < nlayers)
            pend = None
            for m in range(11):
                slab, slabb = next_slab((i, "f"))
                for q in range(2):
                    f = 2 * m + q
                    for nt in range(2):
                        sl = slice(nt * 512, (nt + 1) * 512)
                        gt, gtb = proj_group(slab, slabb, q * 256, 8, 512, h_rhs, hb, nt)
                        up, upb = proj_group(slab, slabb, q * 256 + 128, 8, 512, h_rhs, hb, nt)
                        k = nxt("sg", 2)
                        P.op("act", lambda e: e.activation(sg_t[:, k, :], gt, AF.Silu), reads=[gtb], writes=[sgb[k]])
                        P.op("dve", lambda e: e.tensor_tensor(hid(f, sl), up, sg_t[:, k, :], ALU.mult),
                             reads=[upb, sgb[k]], writes=[hidb[f][nt]])
                if domod:
                    nxt_p = mod_A(i + 1, m)
                    if pend is not None:
                        mod_B(i + 1, *pend)
                    pend = nxt_p
            for c in range(8):
                slab, slabb = next_slab((i, "f"))
                for nt in range(2):
                    bk, bkb = proj_group(slab, slabb, 0, FC, 128, hid_rhs, hidb, nt)
                    resid_update(bk, bkb, c, nt, g2[:, c:c + 1], [modb[i]])
                if domod and c == 0:
                    nxt_p = mod_A(i + 1, 11)
                    mod_B(i + 1, *pend)
                    pend = nxt_p
                if domod and c == 1:
                    mod_B(i + 1, *pend)
                    mod_finish(i + 1)

        for s in range(nsup):
            if s > 0:
                load_x(s)
            for i in range(nlayers):
                kind = i % 3
                deps = [derb[i], modb[i]]
                norm_phase(der[:, i, 0:8], modT[:, i, 0:8], "h32" if kind == 2 else "h", deps)
                if kind == 0:
                    mixer_A(i, s)
                elif kind == 1:
                    mixer_B(i, s)
                else:
                    mixer_C(i, s)
                norm_phase(der[:, i, 8:16], modT[:, i, 24:32], "h", deps)
                ffn(i, s)
            norm_phase(vecs[:, VOFF["fg"]:VOFF["fg"] + 8], None, "out", [vecsb], s)
        assert wstate["used"] == len(seq)
        P.q["sp"].append(([(f"ot{tk}", P.cnt[f"ot{tk}"]) for tk in range(4) if P.cnt.get(f"ot{tk}", 0) > 0], None, None, 0))
        _emit(P, nc, sems)
    return nc


def _emit(P, nc, sems):
    with nc.Block() as block:
        def make(q):
            def body(eng):
                for waits, fn, key, inc in q:
                    for k, v in waits:
                        eng.wait_ge(sems[k], v)
                    if fn is None:
                        continue
                    ins = fn(eng)
                    ins.then_inc(sems[key], inc)
            return body

        block.tensor(make(P.q["pe"]))
        block.scalar(make(P.q["act"]))
        block.vector(make(P.q["dve"]))
        block.gpsimd(make(P.q["pool"]))
        block.sync(make(P.q["sp"]))


_CACHE = {}


def kernel(**inputs):
    inp = {k: np.asarray(v) for k, v in inputs.items()}
    nlayers = NLAYERS
    wpack = _pack_weights(inp)
    adaw = np.ascontiguousarray(inp["ada_w"], dtype=np.float32)
    in_maps = []
    for b in range(8):
        in_maps.append({
            "xT": np.ascontiguousarray(inp["x"][b].T, dtype=np.float32),
            "vecs": _pack_vecs(inp, b),
            "adaw": adaw,
            "wpack": wpack,
            "ident": np.eye(128, dtype=np.float32),
        })
    nc = build(nlayers=nlayers)
    res = run_bass_kernel_spmd(nc, in_maps, core_ids=list(range(8)))
    out = np.stack([np.ascontiguousarray(np.asarray(r["outT"]).T) for r in res.results], axis=0)
    return out.astype(np.float32)
```

```python
import contextlib
import types
import numpy as np
import concourse.bass as bass
import concourse.mybir as mybir
from concourse.bass_utils import run_bass_kernel_spmd

F32 = mybir.dt.float32
BF16 = mybir.dt.bfloat16
ALU = mybir.AluOpType
AF = mybir.ActivationFunctionType

D = 1024
S = 4096
NT = 1024
NSUP = S // NT
F = 2816
FC = F // 128
EPS = 1e-6
NRING = 4
SLOT = 4096
PREFETCH = 3
NLAYERS = 4


def _vec_layout():
    off = {}
    n = 0

    def add(name, w=8):
        nonlocal n
        off[name] = n
        n += w

    for i in range(4):
        add(f"nmg{i}")
        add(f"nfg{i}")
        add(f"adab{i}", 48)
    for j in range(2):
        add(f"acw{j}", 24)
    add("bb1", 16)
    add("bcw", 248)
    add("bcb")
    add("blg")
    add("blb")
    add("bb2")
    add("psc")
    add("fg")
    add("invcnt", 64)
    add("ct", 8)
    return off, n


VOFF, NV = _vec_layout()


def _layer_slabs(i):
    kind = i % 3
    if kind == 0:
        mixer = [("ain", 3072)] * 8 + [("aout", 4096)] * 2
    elif kind == 1:
        mixer = [("bpw1", 4096)] * 4 + [("bpw2", 4096)] * 2
    else:
        mixer = [("cgrp", 2048)]
    ffn = [("fin", 4096)] * 11 + [("fout", 2816)] * 8
    return mixer, ffn


def _slab_table():
    tab = []
    off = 0
    for i in range(4):
        mixer, ffn = _layer_slabs(i)
        m0 = off
        ml = []
        for _, L in mixer:
            ml.append((off, L))
            off += L
        f0 = off
        fl = []
        for _, L in ffn:
            fl.append((off, L))
            off += L
        tab.append(dict(mixer=ml, ffn=fl, mreg=(m0, f0), freg=(f0, off)))
    return tab, off


SLABS, TOTL = _slab_table()


def _colvec(v):
    v = np.asarray(v, np.float32).reshape(-1, 128)
    return np.ascontiguousarray(v.T)


def _pack_cols(w, cols):
    K = w.shape[0]
    sub = w[:, cols]
    n = sub.shape[1]
    return sub.reshape(K // 128, 128, n).transpose(1, 0, 2).reshape(128, (K // 128) * n)


def _pack_weights(inp):
    ar = np.arange
    parts = []
    for i in range(4):
        kind, j = i % 3, i // 3
        if kind == 0:
            w = inp["a_w_in"][j]
            for c in range(8):
                cols = np.concatenate([1024 + c * 128 + ar(128), 2048 + c * 128 + ar(128), c * 128 + ar(128)])
                parts.append(_pack_cols(w, cols))
            w = inp["a_w_out"][j]
            for m in range(2):
                parts.append(_pack_cols(w, m * 512 + ar(512)))
        elif kind == 1:
            w = inp["b_w_pw1"][j]
            for m in range(4):
                cols = np.concatenate([(2 * m) * 128 + ar(128), 1024 + (2 * m) * 128 + ar(128),
                                       (2 * m + 1) * 128 + ar(128), 1024 + (2 * m + 1) * 128 + ar(128)])
                parts.append(_pack_cols(w, cols))
            w = inp["b_w_pw2"][j]
            for m in range(2):
                parts.append(_pack_cols(w, m * 512 + ar(512)))
        else:
            w = inp["p_w_grp"][j]
            parts.append(np.concatenate([_pack_cols(w[g], ar(256)) for g in range(4)], axis=1))
        w = inp["ffn_w_in"][i]
        for m in range(11):
            cols = np.concatenate([(2 * m) * 128 + ar(128), F + (2 * m) * 128 + ar(128),
                                   (2 * m + 1) * 128 + ar(128), F + (2 * m + 1) * 128 + ar(128)])
            parts.append(_pack_cols(w, cols))
        w = inp["ffn_w_out"][i]
        for c in range(8):
            parts.append(_pack_cols(w, c * 128 + ar(128)))
    out = np.ascontiguousarray(np.concatenate(parts, axis=1), dtype=np.float32)
    assert out.shape == (128, TOTL), out.shape
    return out


def _pack_vecs(inp, b):
    v = np.zeros((128, NV), np.float32)

    def put(name, arr):
        arr = np.asarray(arr, np.float32)
        v[:, VOFF[name]:VOFF[name] + arr.shape[1]] = arr

    for i in range(4):
        put(f"nmg{i}", _colvec(inp["norm_mix_g"][i]))
        put(f"nfg{i}", _colvec(inp["norm_ffn_g"][i]))
        put(f"adab{i}", _colvec(inp["ada_b"][i]))
    for j in range(2):
        put(f"acw{j}", np.concatenate([_colvec(inp["a_conv_w"][j][k]) for k in range(3)], axis=1))
    put("bb1", _colvec(inp["b_b_pw1"][0]))
    put("bcw", np.concatenate([_colvec(inp["b_conv_w"][0][k]) for k in range(31)], axis=1))
    put("bcb", _colvec(inp["b_conv_b"][0]))
    put("blg", _colvec(inp["b_ln_g"][0]))
    put("blb", _colvec(inp["b_ln_b"][0]))
    put("bb2", _colvec(inp["b_b_pw2"][0]))
    put("psc", _colvec(inp["p_scale"][0]))
    put("fg", _colvec(inp["final_g"]))
    ic = np.zeros((4, 16), np.float32)
    for g, w in enumerate((2, 4, 8, 16)):
        ic[g] = 1.0 / np.minimum(np.arange(16) + 1, w)
    put("invcnt", np.broadcast_to(ic.reshape(1, 64), (128, 64)))
    put("ct", _colvec(inp["c"][b]))
    return v


class Buf:
    __slots__ = ("name", "w", "r")

    def __init__(self, name):
        self.name = name
        self.w = {}
        self.r = {}


def _snapshot(fn):
    if fn is None or fn.__closure__ is None:
        return fn
    cells = []
    for c in fn.__closure__:
        try:
            v = c.cell_contents
            if isinstance(v, types.FunctionType):
                v = _snapshot(v)
            cells.append(types.CellType(v))
        except ValueError:
            cells.append(c)
    return types.FunctionType(fn.__code__, fn.__globals__, fn.__name__, fn.__defaults__, tuple(cells))


class Prog:
    ENGS = ("pe", "act", "dve", "pool", "sp")

    def __init__(self):
        self.q = {e: [] for e in self.ENGS}
        self.cnt = {}
        self.seen = {e: {} for e in self.ENGS}

    def op(self, eng, fn, reads=(), writes=(), sem=None, inc=1, after=()):
        need = {}
        for b in after:
            for k, v in b.w.items():
                if need.get(k, 0) < v:
                    need[k] = v
        for b in reads:
            for k, v in b.w.items():
                if need.get(k, 0) < v:
                    need[k] = v
        for b in writes:
            for k, v in b.w.items():
                if need.get(k, 0) < v:
                    need[k] = v
            for k, v in b.r.items():
                if need.get(k, 0) < v:
                    need[k] = v
        seen = self.seen[eng]
        waits = []
        for k, v in need.items():
            if eng == "pe" and k == "pe":
                continue
            if seen.get(k, 0) >= v:
                continue
            seen[k] = v
            waits.append((k, v))
        fn = _snapshot(fn)
        key = sem if sem is not None else eng
        self.cnt[key] = self.cnt.get(key, 0) + inc
        val = self.cnt[key]
        self.q[eng].append((waits, fn, key, inc))
        for b in reads:
            if b.r.get(key, 0) < val:
                b.r[key] = val
        for b in writes:
            b.w = {key: val}
            b.r = {}
        return (key, val)


ADA_OFF = TOTL
WBF_L = TOTL


def _slab_order(nlayers, nsup):
    seq = []
    for s in range(nsup):
        for i in range(nlayers):
            for (o, L) in SLABS[i]["mixer"]:
                seq.append((i, "m", o, L))
            for (o, L) in SLABS[i]["ffn"]:
                seq.append((i, "f", o, L))
    return seq


CH = 8192
LAG = 2


def build(nlayers=4, nsup=NSUP):
    nc = bass.Bass("TRN2", target_bir_lowering=False)
    xT = nc.dram_tensor("xT", [D, S], F32, kind="ExternalInput").ap()
    vecs_d = nc.dram_tensor("vecs", [128, NV], F32, kind="ExternalInput").ap()
    adaw = nc.dram_tensor("adaw", [4, D, 6 * D], F32, kind="ExternalInput").ap()
    wpack = nc.dram_tensor("wpack", [128, TOTL], F32, kind="ExternalInput").ap()
    ident_d = nc.dram_tensor("ident", [128, 128], F32, kind="ExternalInput").ap()
    outT = nc.dram_tensor("outT", [D, S], F32, kind="ExternalOutput").ap()
    wbf = nc.dram_tensor("wbf", [128, WBF_L], BF16).ap()

    P = Prog()
    es = contextlib.ExitStack()
    with es:
        def sb(name, shape, dt):
            return es.enter_context(nc.sbuf_tensor(name, shape, dt))

        xres = sb("xres", [128, 8, NT], F32)
        hbuf = sb("hbuf", [128, 8, NT], BF16)
        hid2 = sb("hid", [128, FC * NT], BF16)
        v32f = sb("v32", [128, 16 * 528], F32)
        ubf = sb("ubf", [128, 3, 544], BF16)
        dg = sb("dg", [128, 2, 31 * 128], BF16)
        wring = sb("wring", [128, NRING, SLOT], BF16)
        sqr2 = sb("sqr2", [128, 2, 2, 512], BF16)
        rstd_t = sb("rstd", [128, 2, 512], F32)
        tmp_t = sb("tmp", [128, 4, 512], F32)
        sg_t = sb("sg", [128, 2, 512], F32)
        rows_t = sb("rows", [1, 2, 512], F32)
        vecs = sb("vecs_sb", [128, NV], F32)
        modT = sb("modT", [128, 4, 48], F32)
        der = sb("der", [128, 4, 32], F32)
        cact = sb("cact", [128, 8], BF16)
        haloA = sb("haloA", [128, 2, 8, 2], F32)
        haloB = sb("haloB", [128, 8, 30], BF16)
        haloC = sb("haloC", [128, 8, 15], F32)
        ones_bf = sb("ones_bf", [128, 128], BF16)
        ones_f = sb("ones_f", [128, 128], F32)
        ident_bf = sb("ident_bf", [128, 128], BF16)
        one11 = sb("one11", [1, 1], F32)
        small = sb("small", [128, 64], F32)
        psum = es.enter_context(nc.psum_tensor("psum", [128, 8, 512], F32))

        def hid(f, nt):
            return hid2[:, f * NT + nt * 512:f * NT + (nt + 1) * 512]

        def tsl(nt):
            return slice(nt * 512, (nt + 1) * 512)

        semnames = ["pe", "act", "dve", "pool", "xin0", "xin1", "vec", "idn"]
        semnames += [f"w{k}" for k in range(NRING)]
        semnames += [f"ot{k}" for k in range(4)]
        cvsem = {}
        for i_ in range(4):
            for x_ in "mf":
                a_, b_ = SLABS[i_]["mreg" if x_ == "m" else "freg"]
                for p_ in range((b_ - a_ + 8191) // 8192):
                    cvsem[(i_, x_, p_)] = f"cv{x_}{i_}_{p_}"
        semnames += list(cvsem.values())
        semnames += [f"ad{k}" for k in range(4)]
        sems = {n: es.enter_context(nc.semaphore(n)) for n in semnames}

        xb = [[Buf(f"x{c}_{t}") for t in range(2)] for c in range(8)]
        hb = [[Buf(f"h{c}_{t}") for t in range(2)] for c in range(8)]
        hidb = [[Buf(f"hid{c}_{t}") for t in range(2)] for c in range(FC)]
        vtb = [Buf(f"vt{t}") for t in range(16)]
        ubfb = [Buf(f"ubf{k}") for k in range(3)]
        dgb = [Buf(f"dg{k}") for k in range(2)]
        wb = [Buf(f"w{k}") for k in range(NRING)]
        sq2b = [[Buf(f"sq{t}_{k}") for k in range(2)] for t in range(2)]
        rstdb = [Buf(f"rstd{k}") for k in range(2)]
        tmpb = [Buf(f"tmp{k}") for k in range(4)]
        sgb = [Buf(f"sg{k}") for k in range(2)]
        rowsb = [Buf(f"rows{k}") for k in range(2)]
        bankb = [Buf(f"bank{k}") for k in range(8)]
        vecsb = Buf("vecs")
        constb = Buf("const")
        modb = [Buf(f"mod{i}") for i in range(4)]
        derb = [Buf(f"der{i}") for i in range(4)]
        cactb = Buf("cact")
        haloAb = [[Buf(f"hA{l}_{c}") for c in range(8)] for l in range(2)]
        haloBb = [Buf(f"hB{c}") for c in range(8)]
        haloCb = [Buf(f"hC{c}") for c in range(8)]
        smallb = Buf("small")
        cvb = {k_: Buf(v_) for k_, v_ in cvsem.items()}

        ctr = dict(bank=0, sq=0, rstd=0, tmp=0, sg=0, vt=0, st=0, ubf=0, dg=0, rows=0)

        def nxt(name, n):
            k = ctr[name] % n
            ctr[name] += 1
            return k

        NBANK = 6

        def bank():
            k = nxt("bank", NBANK)
            return psum[:, k, :], bankb[k]

        def statbank(nt):
            return psum[:, 6 + nt, :], bankb[6 + nt]

        def vcol(name, c=0, w=1):
            o = VOFF[name] + c
            return vecs[:, o:o + w]

        def vtile():
            t = nxt("vt", 16)
            return v32f[:, t * 528:(t + 1) * 528], [vtb[t]]

        def stile():
            k = nxt("st", 6)
            ap = hid2[:, 3 * k * 512:3 * k * 512 + 1536].bitcast(F32)
            return ap, [hidb[(3 * k + q) // 2][(3 * k + q) % 2] for q in range(3)]

        P.op("sp", lambda e: e.dma_start(out=vecs[:, :], in_=vecs_d[:, :]), writes=[vecsb], sem="vec", inc=16)
        P.op("pool", lambda e: e.memset(ones_bf[:, :], 1.0 / D), writes=[constb])
        P.op("pool", lambda e: e.memset(ones_f[:, :], 1.0 / D), writes=[constb])
        P.op("pool", lambda e: e.memset(one11[:, :], 1.0), writes=[constb])
        P.op("pool", lambda e: e.memset(small[:, 40:41], EPS), writes=[constb])

        def load_x(s, nt):
            for c in range(8):
                P.op("pool", lambda e: e.dma_start(out=xres[:, c, tsl(nt)],
                                                    in_=xT[c * 128:(c + 1) * 128, s * NT + nt * 512:s * NT + (nt + 1) * 512]),
                     writes=[xb[c][nt]], sem=f"xin{nt}", inc=16)
            v = P.cnt[f"xin{nt}"]
            for c in range(8):
                xb[c][nt].w = {f"xin{nt}": v}


        def pieces_of(i, key):
            a_, b_ = SLABS[i]["mreg" if key == "m" else "freg"]
            out = []
            pos = a_
            while pos < b_:
                out.append((pos, min(pos + CH, b_)))
                pos = out[-1][1]
            return out

        def convert_pieces(i, key, sel=None, deps=()):
            for p, (p0, p1) in enumerate(pieces_of(i, key)):
                if sel is not None and p not in sel:
                    continue
                b = cvb[(i, key, p)]
                P.op("pool", lambda e: e.dma_start(out=wbf[:, p0:p1], in_=wpack[:, p0:p1]),
                     after=list(deps), writes=[b], sem=cvsem[(i, key, p)], inc=16)

        def marker(eng):
            m = Buf("marker")
            if P.cnt.get(eng, 0) > 0:
                m.w = {eng: P.cnt[eng]}
            return m

        P.op("pool", lambda e: e.dma_start(out=ident_bf[:, :], in_=ident_d[:, :]), writes=[constb], sem="idn", inc=16)
        constb.w = {"idn": P.cnt["idn"], "pool": P.cnt["pool"]}
        convert_pieces(0, "m", sel=(0, 1))
        load_x(0, 0)
        load_x(0, 1)

        P.op("act", lambda e: e.activation(cact[:, :], vcol("ct", 0, 8), AF.Silu), reads=[vecsb], writes=[cactb])

        seq = _slab_order(nlayers, nsup)
        wstate = dict(loaded=0, used=0, done=0)

        def issue_loads():
            while wstate["loaded"] < len(seq) and wstate["loaded"] < wstate["done"] + NRING:
                g = wstate["loaded"]
                i, part, o, L = seq[g]
                k = g % NRING
                a_ = SLABS[i]["mreg" if part == "m" else "freg"][0]
                pcs = range((o - a_) // CH, (o + L - 1 - a_) // CH + 1)
                P.op("sp", lambda e: e.dma_start(out=wring[:, k, 0:L], in_=wbf[:, o:o + L]),
                     reads=[cvb[(i, part, p)] for p in pcs], writes=[wb[k]], sem=f"w{k}", inc=16)
                wstate["loaded"] += 1

        def next_slab(expect):
            g = wstate["used"]
            assert seq[g][0:2] == expect, (g, seq[g], expect)
            issue_loads()
            assert wstate["loaded"] > g, ("slab not loadable", g, wstate)
            wstate["used"] += 1
            k = g % NRING
            return wring[:, k, :], wb[k]

        NE = 24
        adastate = dict(loaded=0, n=0)
        stageb = lambda k: [vtb[4 * k + q] for q in range(4)]

        def stage(k):
            return v32f[:, k * 2112:k * 2112 + 2048]

        def ada_issue(i, upto):
            while adastate["loaded"] <= min(upto, NE - 1):
                e_ = adastate["loaded"]
                k = (adastate["n"] + e_) % 4
                src_ap = adaw[i].rearrange("(dc p) e -> p dc e", p=128)[:, :, e_ * 256:(e_ + 1) * 256]
                dst_ap = stage(k).rearrange("p (dc e) -> p dc e", dc=8)
                P.op("sp", lambda e: e.dma_start(out=dst_ap, in_=src_ap), writes=stageb(k), sem=f"ad{k}", inc=16)
                adastate["loaded"] += 1

        def mod_cast(i, e_):
            ada_issue(i, e_ + 3)
            k = (adastate["n"] + e_) % 4
            dk = e_ % 2
            P.op("dve", lambda e: e.tensor_copy(dg[:, dk, 0:2048], stage(k)), reads=stageb(k), writes=[dgb[dk]])

        def mod_mv(i, e_):
            dk = e_ % 2
            bk, bkb = bank()

            def mv(pe):
                for dc in range(8):
                    ins = pe.matmul(bk[0:1, 0:256], cact[:, dc:dc + 1], dg[:, dk, dc * 256:(dc + 1) * 256], start=(dc == 0), stop=(dc == 7))
                return ins

            P.op("pe", mv, reads=[dgb[dk], cactb], writes=[bkb])
            rk = e_ % 2
            P.op("act", lambda e: e.activation(rows_t[0:1, rk, 0:256], bk[0:1, 0:256], AF.Copy), reads=[bkb], writes=[rowsb[rk]])

        def mod_tr(i, e_):
            rk = e_ % 2
            bk2, bkb2 = bank()

            def tr(pe):
                for q in range(2):
                    ins = pe.matmul(bk2[:, q:q + 1], rows_t[0:1, rk, q * 128:(q + 1) * 128], one11[0:1, 0:1], start=True, stop=True)
                return ins

            P.op("pe", tr, reads=[rowsb[rk], constb], writes=[bkb2])
            P.op("dve", lambda e: e.tensor_tensor(modT[:, i, e_ * 2:(e_ + 1) * 2], bk2[:, 0:2], vcol(f"adab{i}", e_ * 2, 2), ALU.add),
                 reads=[bkb2, vecsb], writes=[modb[i]])

        def mod_step(i, t):
            if t == 0:
                adastate["loaded"] = 0
            if t < NE:
                mod_cast(i, t)
            if 1 <= t <= NE:
                mod_mv(i, t - 1)
            if 2 <= t <= NE + 1:
                mod_tr(i, t - 2)
            if t == NE + 1:
                adastate["n"] += NE
                mod_finish(i)

        def mod_finish(i):
            kind = i % 3
            P.op("dve", lambda e: e.scalar_tensor_tensor(der[:, i, 0:8], modT[:, i, 8:16], 1.0, vcol(f"nmg{i}", 0, 8),
                                                          ALU.add, ALU.mult), reads=[modb[i], vecsb], writes=[derb[i]])
            P.op("dve", lambda e: e.scalar_tensor_tensor(der[:, i, 8:16], modT[:, i, 32:40], 1.0, vcol(f"nfg{i}", 0, 8),
                                                          ALU.add, ALU.mult), reads=[modb[i], vecsb], writes=[derb[i]])
            if kind == 2:
                P.op("dve", lambda e: e.tensor_tensor(der[:, i, 16:24], modT[:, i, 16:24], vcol("psc", 0, 8), ALU.mult),
                     reads=[modb[i], vecsb, derb[i]], writes=[derb[i]])
            if kind == 1:
                P.op("dve", lambda e: e.tensor_tensor(der[:, i, 16:24], modT[:, i, 16:24], vcol("bb2", 0, 8), ALU.mult),
                     reads=[modb[i], vecsb, derb[i]], writes=[derb[i]])

        for t in range(NE + 2):
            mod_step(0, t)
            if t == 12:
                convert_pieces(0, "m", sel=(2, 3), deps=[marker("dve")])
        convert_pieces(0, "f", deps=[marker("dve")])

        units = []

        def unit(fn, nslabs=0):
            units.append(dict(emit=fn, nslabs=nslabs, st={}))

        pend_stats = {0: [], 1: []}
        sqctr = [0, 0]

        def flush_stats(nt):
            while pend_stats[nt]:
                pend_stats[nt].pop(0)()

        def stats_chunk(c, nt, defer=True):
            bk, bkb = statbank(nt)
            k = sqctr[nt] % 2
            sqctr[nt] += 1
            P.op("act", lambda e: e.activation(sqr2[:, nt, k, :], xres[:, c, tsl(nt)], AF.Square), reads=[xb[c][nt]], writes=[sq2b[nt][k]])

            def mm():
                P.op("pe", lambda e: e.matmul(bk, ones_bf[:, :], sqr2[:, nt, k, :], start=(c == 0), stop=(c == 7)),
                     reads=[sq2b[nt][k], constb], writes=[bkb])

            if defer:
                pend_stats[nt].append(mm)
            else:
                mm()

        def U_stats():
            def emit(nt, st):
                for c in range(8):
                    stats_chunk(c, nt, defer=False)
            unit(emit)

        def U_norm(gs, sh, mode, deps, s=0):
            def chunks(nt, st, c0):
                rk = st[("rk", nt)]
                for c in range(c0, c0 + 4):
                    tk = nxt("tmp", 4)
                    P.op("dve", lambda e: e.scalar_tensor_tensor(
                        tmp_t[:, tk, :], xres[:, c, tsl(nt)], gs[:, c:c + 1], rstd_t[:, rk, :], ALU.mult, ALU.mult),
                        reads=[xb[c][nt], rstdb[rk]] + deps, writes=[tmpb[tk]])
                    if mode == "out":
                        P.op("sp", lambda e: e.dma_start(
                            out=outT[c * 128:(c + 1) * 128, s * NT + nt * 512: s * NT + (nt + 1) * 512], in_=tmp_t[:, tk, :]),
                            reads=[tmpb[tk]], sem=f"ot{tk}", inc=16)
                    elif mode == "h":
                        P.op("act", lambda e: e.activation(hbuf[:, c, tsl(nt)], tmp_t[:, tk, :], AF.Identity,
                                                            bias=sh[:, c:c + 1], scale=1.0),
                             reads=[tmpb[tk]] + deps, writes=[hb[c][nt]])
                    else:
                        P.op("act", lambda e: e.activation(v32f[:, (c * 2 + nt) * 528 + 15:(c * 2 + nt) * 528 + 527], tmp_t[:, tk, :], AF.Identity,
                                                            bias=sh[:, c:c + 1], scale=1.0),
                             reads=[tmpb[tk]] + deps, writes=[vtb[c * 2 + nt]])

            def emit_a(nt, st):
                flush_stats(nt)
                bk, bkb = statbank(nt)
                rk = nt
                st[("rk", nt)] = rk
                P.op("act", lambda e: e.activation(rstd_t[:, rk, :], bk, AF.Sqrt, bias=small[:, 40:41], scale=1.0),
                     reads=[bkb, constb], writes=[rstdb[rk]])
                P.op("dve", lambda e: e.reciprocal(rstd_t[:, rk, :], rstd_t[:, rk, :]), reads=[rstdb[rk]], writes=[rstdb[rk]])
                chunks(nt, st, 0)

            u0 = dict(emit=emit_a, nslabs=0, st={})
            units.append(u0)
            units.append(dict(emit=lambda nt, st: chunks(nt, u0["st"], 4), nslabs=0, st={}))

        def resid_update(bk, bkb, c, nt, gvec, deps, bias=None):
            flush_stats(nt)
            P.op("dve", lambda e: e.scalar_tensor_tensor(xres[:, c, tsl(nt)], bk, gvec, xres[:, c, tsl(nt)], ALU.mult, ALU.add),
                 reads=[bkb] + deps, writes=[xb[c][nt]])
            if bias is not None:
                bap, bdeps = bias
                P.op("dve", lambda e: e.tensor_scalar(xres[:, c, tsl(nt)], xres[:, c, tsl(nt)], bap, None, ALU.add),
                     reads=[xb[c][nt]] + bdeps, writes=[xb[c][nt]])
            stats_chunk(c, nt)

        def proj_group(slab, slabb, col0, kchunks, stride, rhs_fn, rhs_bufs, nt):
            bk, bkb = bank()

            def fn(pe):
                for k in range(kchunks):
                    ins = pe.matmul(bk, slab[:, k * stride + col0:k * stride + col0 + 128], rhs_fn(k, nt),
                                    start=(k == 0), stop=(k == kchunks - 1))
                return ins

            P.op("pe", fn, reads=[slabb] + [rhs_bufs[k][nt] for k in range(kchunks)], writes=[bkb])
            return bk, bkb

        h_rhs = lambda k, nt: hbuf[:, k, tsl(nt)]
        hid_rhs = lambda k, nt: hid(k, nt)

        def get_slab(nt, st, expect):
            if nt == 0:
                st["slab"] = next_slab(expect)
            return st["slab"]

        def units_A(i, s):
            la = i // 3
            g1 = modT[:, i, 16:24]

            def mk_in(j):
                def emit(nt, st):
                    slab, slabb = get_slab(nt, st, (i, "m"))
                    U, Ub = vtile()
                    if s == 0 and nt == 0:
                        P.op("dve", lambda e: e.memset(U[:, 0:2], 0.0), writes=Ub)
                    else:
                        P.op("dve", lambda e: e.tensor_copy(U[:, 0:2], haloA[:, la, j, :]), reads=[haloAb[la][j]], writes=Ub)
                    gc, gcb = proj_group(slab, slabb, 0, 8, 384, h_rhs, hb, nt)
                    hv, hvb = proj_group(slab, slabb, 128, 8, 384, h_rhs, hb, nt)
                    gb, gbb = proj_group(slab, slabb, 256, 8, 384, h_rhs, hb, nt)
                    k = nxt("sg", 2)
                    P.op("act", lambda e: e.activation(sg_t[:, k, :], hv, AF.Copy), reads=[hvb], writes=[sgb[k]])
                    P.op("dve", lambda e: e.tensor_tensor(U[:, 2:514], gc, sg_t[:, k, :], ALU.mult), reads=[gcb, sgb[k]], writes=Ub)
                    tk = nxt("tmp", 4)
                    P.op("act", lambda e: e.activation(tmp_t[:, tk, :], gb, AF.Copy), reads=[gbb], writes=[tmpb[tk]])
                    P.op("dve", lambda e: e.tensor_copy(haloA[:, la, j, :], U[:, 512:514]), reads=Ub, writes=[haloAb[la][j]])
                    V, Vb = vtile()
                    w = lambda kk: vcol(f"acw{la}", kk * 8 + j, 1)
                    P.op("dve", lambda e: e.tensor_scalar(V[:, 0:512], U[:, 0:512], w(0), None, ALU.mult), reads=Ub + [vecsb], writes=Vb)
                    P.op("dve", lambda e: e.scalar_tensor_tensor(V[:, 0:512], U[:, 1:513], w(1), V[:, 0:512], ALU.mult, ALU.add),
                         reads=Ub + Vb, writes=Vb)
                    P.op("dve", lambda e: e.scalar_tensor_tensor(V[:, 0:512], U[:, 2:514], w(2), V[:, 0:512], ALU.mult, ALU.add),
                         reads=Ub + Vb, writes=Vb)
                    P.op("dve", lambda e: e.tensor_tensor(hid(j, nt), tmp_t[:, tk, :], V[:, 0:512], ALU.mult),
                         reads=[tmpb[tk]] + Vb, writes=[hidb[j][nt]])
                unit(emit, 1)

            def mk_out(m):
                def emit(nt, st):
                    slab, slabb = get_slab(nt, st, (i, "m"))
                    grp = [proj_group(slab, slabb, q * 128, 8, 512, hid_rhs, hidb, nt) for q in range(4)]
                    for q in range(4):
                        c = m * 4 + q
                        resid_update(grp[q][0], grp[q][1], c, nt, g1[:, c:c + 1], [modb[i]])
                unit(emit, 1)

            for j in range(8):
                mk_in(j)
            for m in range(2):
                mk_out(m)

        def units_B(i, s):
            g1 = modT[:, i, 16:24]

            def mk_in(m):
                def emit(nt, st):
                    slab, slabb = get_slab(nt, st, (i, "m"))
                    convs = []
                    for q in range(2):
                        j = 2 * m + q
                        uk = nxt("ubf", 3)
                        U = ubf[:, uk, :]
                        if s == 0 and nt == 0:
                            P.op("dve", lambda e: e.memset(U[:, 0:30], 0.0), writes=[ubfb[uk]])
                        else:
                            P.op("dve", lambda e: e.tensor_copy(U[:, 0:30], haloB[:, j, :]), reads=[haloBb[j]], writes=[ubfb[uk]])
                        dk = nxt("dg", 2)

                        def mkdiag(e):
                            for kk in range(31):
                                ins = e.tensor_scalar(dg[:, dk, kk * 128:(kk + 1) * 128], ident_bf[:, :], vcol("bcw", kk * 8 + j, 1), None, ALU.mult)
                            return ins

                        P.op("dve", mkdiag, reads=[vecsb, constb], writes=[dgb[dk]])
                        a, ab = proj_group(slab, slabb, q * 256, 8, 512, h_rhs, hb, nt)
                        gt, gtb = proj_group(slab, slabb, q * 256 + 128, 8, 512, h_rhs, hb, nt)
                        k = nxt("sg", 2)
                        P.op("act", lambda e: e.activation(sg_t[:, k, :], gt, AF.Sigmoid, bias=vcol("bb1", 8 + j, 1), scale=1.0),
                             reads=[gtb, vecsb], writes=[sgb[k]])
                        P.op("dve", lambda e: e.scalar_tensor_tensor(U[:, 30:542], a, vcol("bb1", j, 1), sg_t[:, k, :], ALU.add, ALU.mult),
                             reads=[ab, sgb[k], vecsb], writes=[ubfb[uk]])
                        P.op("dve", lambda e: e.tensor_copy(haloB[:, j, :], U[:, 512:542]), reads=[ubfb[uk]], writes=[haloBb[j]])
                        convs.append((j, uk, dk, U))
                    for (j, uk, dk, U) in convs:
                        bk, bkb = bank()

                        def conv(pe):
                            for kk in range(31):
                                ins = pe.matmul(bk, dg[:, dk, kk * 128:(kk + 1) * 128], U[:, kk:kk + 512], start=(kk == 0), stop=(kk == 30))
                            return ins

                        P.op("pe", conv, reads=[dgb[dk], ubfb[uk]], writes=[bkb])
                        P.op("act", lambda e: e.activation(v32f[:, (j * 2 + nt) * 528 + 0:(j * 2 + nt) * 528 + 512], bk, AF.Identity, bias=vcol("bcb", j, 1), scale=1.0),
                             reads=[bkb, vecsb], writes=[vtb[j * 2 + nt]])
                unit(emit, 1)

            def ln_chunks(nt, st, c0):
                k, rk = st[("ln", nt)]
                mus = sg_t[:, k, :]
                R = rstd_t[:, rk, :]
                for c in range(c0, c0 + 4):
                    tk = nxt("tmp", 4)
                    T = tmp_t[:, tk, :]
                    P.op("dve", lambda e: e.tensor_tensor(T, v32f[:, (c * 2 + nt) * 528 + 0:(c * 2 + nt) * 528 + 512], mus, ALU.subtract),
                         reads=[vtb[c * 2 + nt], sgb[k]], writes=[tmpb[tk]])
                    P.op("dve", lambda e: e.tensor_tensor(T, T, R, ALU.mult), reads=[tmpb[tk], rstdb[rk]], writes=[tmpb[tk]])
                    P.op("act", lambda e: e.activation(hid(c, nt), T, AF.Silu, bias=vcol("blb", c, 1), scale=vcol("blg", c, 1)),
                         reads=[tmpb[tk], vecsb], writes=[hidb[c][nt]])

            def emit_ln_a(nt, st):
                flush_stats(nt)
                mu, mub = bank()
                ms, msb = bank()
                sqs = []
                for c in range(8):
                    k = c % 2
                    if c >= 2:
                        sqs.pop(0)()
                    P.op("act", lambda e: e.activation(sqr2[:, nt, k, :], v32f[:, (c * 2 + nt) * 528 + 0:(c * 2 + nt) * 528 + 512], AF.Square),
                         reads=[vtb[c * 2 + nt]], writes=[sq2b[nt][k]])
                    P.op("pe", lambda e: e.matmul(mu, ones_f[:, :], v32f[:, (c * 2 + nt) * 528 + 0:(c * 2 + nt) * 528 + 512], start=(c == 0), stop=(c == 7)),
                         reads=[vtb[c * 2 + nt], constb], writes=[mub])

                    def msmm(c=c, k=k):
                        P.op("pe", lambda e: e.matmul(ms, ones_bf[:, :], sqr2[:, nt, k, :], start=(c == 0), stop=(c == 7)),
                             reads=[sq2b[nt][k], constb], writes=[msb])

                    sqs.append(msmm)
                while sqs:
                    sqs.pop(0)()
                k = nxt("sg", 2)
                mus = sg_t[:, k, :]
                P.op("act", lambda e: e.activation(mus, mu, AF.Copy), reads=[mub], writes=[sgb[k]])
                rk = nt
                R = rstd_t[:, rk, :]
                P.op("dve", lambda e: e.tensor_tensor(R, mus, mus, ALU.mult), reads=[sgb[k]], writes=[rstdb[rk]])
                P.op("dve", lambda e: e.tensor_tensor(R, ms, R, ALU.subtract), reads=[msb, rstdb[rk]], writes=[rstdb[rk]])
                P.op("act", lambda e: e.activation(R, R, AF.Sqrt, bias=small[:, 40:41], scale=1.0),
                     reads=[rstdb[rk], constb], writes=[rstdb[rk]])
                P.op("dve", lambda e: e.reciprocal(R, R), reads=[rstdb[rk]], writes=[rstdb[rk]])
                st[("ln", nt)] = (k, rk)
                ln_chunks(nt, st, 0)
                ln_chunks(nt, st, 4)

            def mk_out(m):
                def emit(nt, st):
                    slab, slabb = get_slab(nt, st, (i, "m"))
                    grp = [proj_group(slab, slabb, q * 128, 8, 512, hid_rhs, hidb, nt) for q in range(4)]
                    for q in range(4):
                        c = m * 4 + q
                        resid_update(grp[q][0], grp[q][1], c, nt, g1[:, c:c + 1], [modb[i]], bias=(der[:, i, 16 + c:17 + c], [derb[i]]))
                unit(emit, 1)

            for m in range(4):
                mk_in(m)
            unit(emit_ln_a)
            for m in range(2):
                mk_out(m)

        def units_C(i, s):
            def emit(nt, st):
                slab, slabb = get_slab(nt, st, (i, "m"))
                for c in range(8):
                    g = c // 2
                    w = 2 << g
                    t = c * 2 + nt
                    H = v32f[:, t * 528:(t + 1) * 528]
                    Hb = [vtb[t]]
                    engname = "pool" if (c % 2 == 0 and s > 0) else "dve"
                    if s == 0 and nt == 0:
                        P.op("dve", lambda e: e.memset(H[:, 0:15], 0.0), writes=Hb)
                    else:
                        P.op("dve", lambda e: e.tensor_copy(H[:, 0:15], haloC[:, c, :]), reads=[haloCb[c]], writes=Hb)
                    P.op("dve", lambda e: e.tensor_copy(haloC[:, c, :], H[:, 512:527]), reads=Hb, writes=[haloCb[c]])
                    prev, prevb = H, Hb
                    for l in range(1, g + 2):
                        a = 15 - (w - (1 << l))
                        sh_ = 1 << (l - 1)
                        Sx, Sb = stile()
                        P.op(engname, lambda e: e.tensor_tensor(Sx[:, a:527], prev[:, a:527], prev[:, a - sh_:527 - sh_], ALU.add),
                             reads=prevb, writes=Sb)
                        prev, prevb = Sx, Sb
                    P.op("dve", lambda e: e.scalar_tensor_tensor(hbuf[:, c, tsl(nt)], prev[:, 15:527], 1.0 / w, H[:, 15:527],
                                                                  ALU.mult, ALU.subtract),
                         reads=prevb + Hb, writes=[hb[c][nt]])
                    if s == 0 and nt == 0:
                        P.op("dve", lambda e: e.tensor_tensor(small[:, 0:16], prev[:, 15:31], vcol("invcnt", g * 16, 16), ALU.mult),
                             reads=prevb + [vecsb, smallb], writes=[smallb])
                        P.op("dve", lambda e: e.tensor_tensor(hbuf[:, c, 0:16], small[:, 0:16], H[:, 15:31], ALU.subtract),
                             reads=[smallb] + Hb, writes=[hb[c][0]])
                for half in range(2):
                    grp = []
                    for g in range(2 * half, 2 * half + 2):
                        for m in range(2):
                            bk, bkb = bank()

                            def fn(pe):
                                for kc in range(2):
                                    o = (g * 2 + kc) * 256 + m * 128
                                    ins = pe.matmul(bk, slab[:, o:o + 128], hbuf[:, 2 * g + kc, tsl(nt)], start=(kc == 0), stop=(kc == 1))
                                return ins

                            P.op("pe", fn, reads=[slabb, hb[2 * g][nt], hb[2 * g + 1][nt]], writes=[bkb])
                            grp.append((2 * g + m, bk, bkb))
                    for (c, bk, bkb) in grp:
                        resid_update(bk, bkb, c, nt, der[:, i, 16 + c:17 + c], [derb[i]])
            unit(emit, 1)

        def units_ffn(i, s):
            g2 = modT[:, i, 40:48]
            domod = (s == 0 and i + 1 < nlayers)

            def mk_mod(t):
                def emit(nt, st):
                    if nt == 0:
                        mod_step(i + 1, t)
                unit(emit, 0)

            def mk_in(m):
                def emit(nt, st):
                    slab, slabb = get_slab(nt, st, (i, "f"))
                    for q in range(2):
                        f = 2 * m + q
                        gt, gtb = proj_group(slab, slabb, q * 256, 8, 512, h_rhs, hb, nt)
                        up, upb = proj_group(slab, slabb, q * 256 + 128, 8, 512, h_rhs, hb, nt)
                        k = nxt("sg", 2)
                        P.op("act", lambda e: e.activation(sg_t[:, k, :], gt, AF.Silu), reads=[gtb], writes=[sgb[k]])
                        P.op("dve", lambda e: e.tensor_tensor(hid(f, nt), up, sg_t[:, k, :], ALU.mult),
                             reads=[upb, sgb[k]], writes=[hidb[f][nt]])
                unit(emit, 1)

            def mk_out(c):
                def emit(nt, st):
                    slab, slabb = get_slab(nt, st, (i, "f"))
                    bk, bkb = proj_group(slab, slabb, 0, FC, 128, hid_rhs, hidb, nt)
                    resid_update(bk, bkb, c, nt, g2[:, c:c + 1], [modb[i]])
                unit(emit, 1)

            tq = list(range(NE + 2)) if domod else []
            for m in range(11):
                mk_in(m)
                for _ in range(2):
                    if tq:
                        mk_mod(tq.pop(0))
            for c in range(8):
                mk_out(c)
                if tq:
                    mk_mod(tq.pop(0))
            assert not tq

        for s in range(nsup):
            if s > 0:
                unit(lambda nt, st, s=s: load_x(s, nt))
            U_stats()
            for i in range(nlayers):
                kind = i % 3
                deps = [derb[i], modb[i]]
                U_norm(der[:, i, 0:8], modT[:, i, 0:8], "h32" if kind == 2 else "h", deps)
                n_before = len(units)
                if kind == 0:
                    units_A(i, s)
                elif kind == 1:
                    units_B(i, s)
                else:
                    units_C(i, s)
                if s == 0 and i + 1 < nlayers:
                    def cv_emit(nt, st, i2=i + 1):
                        if nt == 0:
                            mk = marker("pe")
                            convert_pieces(i2, "m", deps=[mk])
                            convert_pieces(i2, "f", deps=[mk])
                    units.insert(n_before + 1, dict(emit=cv_emit, nslabs=0, st={}))
                U_norm(der[:, i, 8:16], modT[:, i, 24:32], "h", deps)
                units_ffn(i, s)
            U_norm(vecs[:, VOFF["fg"]:VOFF["fg"] + 8], None, "out", [vecsb], s)

        for idx in range(len(units) + LAG):
            if idx < len(units):
                u = units[idx]
                u["emit"](0, u["st"])
            j = idx - LAG
            if j >= 0:
                u = units[j]
                u["emit"](1, u["st"])
                wstate["done"] += u["nslabs"]
                issue_loads()
        assert wstate["used"] == len(seq), (wstate, len(seq))
        P.q["sp"].append(([(f"ot{tk}", P.cnt[f"ot{tk}"]) for tk in range(4) if P.cnt.get(f"ot{tk}", 0) > 0], None, None, 0))
        _emit(P, nc, sems)
    return nc


def _emit(P, nc, sems):
    with nc.Block() as block:
        def make(q):
            def body(eng):
                for waits, fn, key, inc in q:
                    for k, v in waits:
                        eng.wait_ge(sems[k], v)
                    if fn is None:
                        continue
                    ins = fn(eng)
                    ins.then_inc(sems[key], inc)
            return body

        block.tensor(make(P.q["pe"]))
        block.scalar(make(P.q["act"]))
        block.vector(make(P.q["dve"]))
        block.gpsimd(make(P.q["pool"]))
        block.sync(make(P.q["sp"]))


_CACHE = {}


def kernel(**inputs):
    inp = {k: np.asarray(v) for k, v in inputs.items()}
    nlayers = NLAYERS
    wpack = _pack_weights(inp)
    adaw = np.ascontiguousarray(inp["ada_w"], dtype=np.float32)
    in_maps = []
    for b in range(8):
        in_maps.append({
            "xT": np.ascontiguousarray(inp["x"][b].T, dtype=np.float32),
            "vecs": _pack_vecs(inp, b),
            "adaw": adaw,
            "wpack": wpack,
            "ident": np.eye(128, dtype=np.float32),
        })
    nc = build(nlayers=nlayers)
    res = run_bass_kernel_spmd(nc, in_maps, core_ids=list(range(8)))
    out = np.stack([np.ascontiguousarray(np.asarray(r["outT"]).T) for r in res.results], axis=0)
    return out.astype(np.float32)
```

```python
import contextlib
import types
import numpy as np
import concourse.bass as bass
import concourse.mybir as mybir
from concourse.bass_utils import run_bass_kernel_spmd

F32 = mybir.dt.float32
BF16 = mybir.dt.bfloat16
ALU = mybir.AluOpType
AF = mybir.ActivationFunctionType

D = 1024
S = 4096
NT = 1024
NSUP = S // NT
F = 2816
FC = F // 128
EPS = 1e-6
NRING = 4
SLOT = 4096
PREFETCH = 3
NLAYERS = 4


def _vec_layout():
    off = {}
    n = 0

    def add(name, w=8):
        nonlocal n
        off[name] = n
        n += w

    for i in range(4):
        add(f"nmg{i}")
        add(f"nfg{i}")
        add(f"adab{i}", 48)
    for j in range(2):
        add(f"acw{j}", 24)
    add("bb1", 16)
    add("bcw", 248)
    add("bcb")
    add("blg")
    add("blb")
    add("bb2")
    add("psc")
    add("fg")
    add("invcnt", 64)
    add("ct", 8)
    return off, n


VOFF, NV = _vec_layout()


def _layer_slabs(i):
    kind = i % 3
    if kind == 0:
        mixer = [("ain", 3072)] * 8 + [("aout", 4096)] * 2
    elif kind == 1:
        mixer = [("bpw1", 4096)] * 4 + [("bpw2", 4096)] * 2
    else:
        mixer = [("cgrp", 2048)]
    ffn = [("fin", 4096)] * 11 + [("fout", 2816)] * 8
    return mixer, ffn


def _slab_table():
    tab = []
    off = 0
    for i in range(4):
        mixer, ffn = _layer_slabs(i)
        m0 = off
        ml = []
        for _, L in mixer:
            ml.append((off, L))
            off += L
        f0 = off
        fl = []
        for _, L in ffn:
            fl.append((off, L))
            off += L
        tab.append(dict(mixer=ml, ffn=fl, mreg=(m0, f0), freg=(f0, off)))
    return tab, off


SLABS, TOTL = _slab_table()


def _colvec(v):
    v = np.asarray(v, np.float32).reshape(-1, 128)
    return np.ascontiguousarray(v.T)


def _pack_cols(w, cols):
    K = w.shape[0]
    sub = w[:, cols]
    n = sub.shape[1]
    return sub.reshape(K // 128, 128, n).transpose(1, 0, 2).reshape(128, (K // 128) * n)


def _pack_weights(inp):
    ar = np.arange
    parts = []
    for i in range(4):
        kind, j = i % 3, i // 3
        if kind == 0:
            w = inp["a_w_in"][j]
            for c in range(8):
                cols = np.concatenate([1024 + c * 128 + ar(128), 2048 + c * 128 + ar(128), c * 128 + ar(128)])
                parts.append(_pack_cols(w, cols))
            w = inp["a_w_out"][j]
            for m in range(2):
                parts.append(_pack_cols(w, m * 512 + ar(512)))
        elif kind == 1:
            w = inp["b_w_pw1"][j]
            for m in range(4):
                cols = np.concatenate([(2 * m) * 128 + ar(128), 1024 + (2 * m) * 128 + ar(128),
                                       (2 * m + 1) * 128 + ar(128), 1024 + (2 * m + 1) * 128 + ar(128)])
                parts.append(_pack_cols(w, cols))
            w = inp["b_w_pw2"][j]
            for m in range(2):
                parts.append(_pack_cols(w, m * 512 + ar(512)))
        else:
            w = inp["p_w_grp"][j]
            parts.append(np.concatenate([_pack_cols(w[g], ar(256)) for g in range(4)], axis=1))
        w = inp["ffn_w_in"][i]
        for m in range(11):
            cols = np.concatenate([(2 * m) * 128 + ar(128), F + (2 * m) * 128 + ar(128),
                                   (2 * m + 1) * 128 + ar(128), F + (2 * m + 1) * 128 + ar(128)])
            parts.append(_pack_cols(w, cols))
        w = inp["ffn_w_out"][i]
        for c in range(8):
            parts.append(_pack_cols(w, c * 128 + ar(128)))
    out = np.ascontiguousarray(np.concatenate(parts, axis=1), dtype=np.float32)
    assert out.shape == (128, TOTL), out.shape
    return out


def _pack_vecs(inp, b):
    v = np.zeros((128, NV), np.float32)

    def put(name, arr):
        arr = np.asarray(arr, np.float32)
        v[:, VOFF[name]:VOFF[name] + arr.shape[1]] = arr

    for i in range(4):
        put(f"nmg{i}", _colvec(inp["norm_mix_g"][i]))
        put(f"nfg{i}", _colvec(inp["norm_ffn_g"][i]))
        put(f"adab{i}", _colvec(inp["ada_b"][i]))
    for j in range(2):
        put(f"acw{j}", np.concatenate([_colvec(inp["a_conv_w"][j][k]) for k in range(3)], axis=1))
    put("bb1", _colvec(inp["b_b_pw1"][0]))
    put("bcw", np.concatenate([_colvec(inp["b_conv_w"][0][k]) for k in range(31)], axis=1))
    put("bcb", _colvec(inp["b_conv_b"][0]))
    put("blg", _colvec(inp["b_ln_g"][0]))
    put("blb", _colvec(inp["b_ln_b"][0]))
    put("bb2", _colvec(inp["b_b_pw2"][0]))
    put("psc", _colvec(inp["p_scale"][0]))
    put("fg", _colvec(inp["final_g"]))
    ic = np.zeros((4, 16), np.float32)
    for g, w in enumerate((2, 4, 8, 16)):
        ic[g] = 1.0 / np.minimum(np.arange(16) + 1, w)
    put("invcnt", np.broadcast_to(ic.reshape(1, 64), (128, 64)))
    put("ct", _colvec(inp["c"][b]))
    return v


class Buf:
    __slots__ = ("name", "w", "r")

    def __init__(self, name):
        self.name = name
        self.w = {}
        self.r = {}


def _snapshot(fn):
    if fn is None or fn.__closure__ is None:
        return fn
    cells = []
    for c in fn.__closure__:
        try:
            v = c.cell_contents
            if isinstance(v, types.FunctionType):
                v = _snapshot(v)
            cells.append(types.CellType(v))
        except ValueError:
            cells.append(c)
    return types.FunctionType(fn.__code__, fn.__globals__, fn.__name__, fn.__defaults__, tuple(cells))


class Prog:
    ENGS = ("pe", "act", "dve", "pool", "sp")

    def __init__(self):
        self.q = {e: [] for e in self.ENGS}
        self.cnt = {}
        self.seen = {e: {} for e in self.ENGS}

    def op(self, eng, fn, reads=(), writes=(), sem=None, inc=1, after=()):
        need = {}
        for b in after:
            for k, v in b.w.items():
                if need.get(k, 0) < v:
                    need[k] = v
        for b in reads:
            for k, v in b.w.items():
                if need.get(k, 0) < v:
                    need[k] = v
        for b in writes:
            for k, v in b.w.items():
                if need.get(k, 0) < v:
                    need[k] = v
            for k, v in b.r.items():
                if need.get(k, 0) < v:
                    need[k] = v
        seen = self.seen[eng]
        waits = []
        for k, v in need.items():
            if eng == "pe" and k == "pe":
                continue
            if seen.get(k, 0) >= v:
                continue
            seen[k] = v
            waits.append((k, v))
        fn = _snapshot(fn)
        key = sem if sem is not None else eng
        self.cnt[key] = self.cnt.get(key, 0) + inc
        val = self.cnt[key]
        self.q[eng].append((waits, fn, key, inc))
        for b in reads:
            if b.r.get(key, 0) < val:
                b.r[key] = val
        for b in writes:
            b.w = {key: val}
            b.r = {}
        return (key, val)


ADA_OFF = TOTL
WBF_L = TOTL


def _slab_order(nlayers, nsup):
    seq = []
    for s in range(nsup):
        for i in range(nlayers):
            for (o, L) in SLABS[i]["mixer"]:
                seq.append((i, "m", o, L))
            for (o, L) in SLABS[i]["ffn"]:
                seq.append((i, "f", o, L))
    return seq


CH = 8192
LAG = 2


def build(nlayers=4, nsup=NSUP):
    nc = bass.Bass("TRN2", target_bir_lowering=False)
    xT = nc.dram_tensor("xT", [D, S], F32, kind="ExternalInput").ap()
    vecs_d = nc.dram_tensor("vecs", [128, NV], F32, kind="ExternalInput").ap()
    adaw = nc.dram_tensor("adaw", [4, D, 6 * D], F32, kind="ExternalInput").ap()
    wpack = nc.dram_tensor("wpack", [128, TOTL], F32, kind="ExternalInput").ap()
    ident_d = nc.dram_tensor("ident", [128, 128], F32, kind="ExternalInput").ap()
    outT = nc.dram_tensor("outT", [D, S], F32, kind="ExternalOutput").ap()
    wbf = nc.dram_tensor("wbf", [128, WBF_L], BF16).ap()

    P = Prog()
    es = contextlib.ExitStack()
    with es:
        def sb(name, shape, dt):
            return es.enter_context(nc.sbuf_tensor(name, shape, dt))

        xres = sb("xres", [128, 8, NT], F32)
        hbuf = sb("hbuf", [128, 8, NT], BF16)
        hid2 = sb("hid", [128, FC * NT], BF16)
        v32f = sb("v32", [128, 16 * 528], F32)
        ubf = sb("ubf", [128, 3, 544], BF16)
        dg = sb("dg", [128, 2, 31 * 128], BF16)
        wring = sb("wring", [128, NRING, SLOT], BF16)
        sqr2 = sb("sqr2", [128, 2, 2, 512], BF16)
        rstd_t = sb("rstd", [128, 2, 512], F32)
        tmp_t = sb("tmp", [128, 4, 512], F32)
        sg_t = sb("sg", [128, 2, 512], F32)
        rows_t = sb("rows", [1, 2, 512], F32)
        vecs = sb("vecs_sb", [128, NV], F32)
        modT = sb("modT", [128, 4, 48], F32)
        der = sb("der", [128, 4, 32], F32)
        cact = sb("cact", [128, 8], BF16)
        haloA = sb("haloA", [128, 2, 8, 2], F32)
        haloB = sb("haloB", [128, 8, 30], BF16)
        haloC = sb("haloC", [128, 8, 15], F32)
        ones_bf = sb("ones_bf", [128, 128], BF16)
        ones_f = sb("ones_f", [128, 128], F32)
        ident_bf = sb("ident_bf", [128, 128], BF16)
        one11 = sb("one11", [1, 1], F32)
        small = sb("small", [128, 64], F32)
        psum = es.enter_context(nc.psum_tensor("psum", [128, 8, 512], F32))

        def hid(f, nt):
            return hid2[:, f * NT + nt * 512:f * NT + (nt + 1) * 512]

        def tsl(nt):
            return slice(nt * 512, (nt + 1) * 512)

        semnames = ["pe", "act", "dve", "pool", "xin0", "xin1", "vec", "idn"]
        semnames += [f"w{k}" for k in range(NRING)]
        semnames += [f"ot{k}" for k in range(4)]
        cvsem = {}
        for i_ in range(4):
            for x_ in "mf":
                a_, b_ = SLABS[i_]["mreg" if x_ == "m" else "freg"]
                for p_ in range((b_ - a_ + 8191) // 8192):
                    cvsem[(i_, x_, p_)] = f"cv{x_}{i_}_{p_}"
        semnames += list(cvsem.values())
        semnames += [f"ad{k}" for k in range(4)]
        sems = {n: es.enter_context(nc.semaphore(n)) for n in semnames}

        xb = [[Buf(f"x{c}_{t}") for t in range(2)] for c in range(8)]
        hb = [[Buf(f"h{c}_{t}") for t in range(2)] for c in range(8)]
        hidb = [[Buf(f"hid{c}_{t}") for t in range(2)] for c in range(FC)]
        vtb = [Buf(f"vt{t}") for t in range(16)]
        ubfb = [Buf(f"ubf{k}") for k in range(3)]
        dgb = [Buf(f"dg{k}") for k in range(2)]
        wb = [Buf(f"w{k}") for k in range(NRING)]
        sq2b = [[Buf(f"sq{t}_{k}") for k in range(2)] for t in range(2)]
        rstdb = [Buf(f"rstd{k}") for k in range(2)]
        tmpb = [Buf(f"tmp{k}") for k in range(4)]
        sgb = [Buf(f"sg{k}") for k in range(2)]
        rowsb = [Buf(f"rows{k}") for k in range(2)]
        bankb = [Buf(f"bank{k}") for k in range(8)]
        vecsb = Buf("vecs")
        constb = Buf("const")
        modb = [Buf(f"mod{i}") for i in range(4)]
        derb = [Buf(f"der{i}") for i in range(4)]
        cactb = Buf("cact")
        haloAb = [[Buf(f"hA{l}_{c}") for c in range(8)] for l in range(2)]
        haloBb = [Buf(f"hB{c}") for c in range(8)]
        haloCb = [Buf(f"hC{c}") for c in range(8)]
        smallb = Buf("small")
        cvb = {k_: Buf(v_) for k_, v_ in cvsem.items()}

        ctr = dict(bank=0, sq=0, rstd=0, tmp=0, sg=0, vt=0, st=0, ubf=0, dg=0, rows=0)

        def nxt(name, n):
            k = ctr[name] % n
            ctr[name] += 1
            return k

        NBANK = 6

        def bank():
            k = nxt("bank", NBANK)
            return psum[:, k, :], bankb[k]

        def statbank(nt):
            return psum[:, 6 + nt, :], bankb[6 + nt]

        def vcol(name, c=0, w=1):
            o = VOFF[name] + c
            return vecs[:, o:o + w]

        def vtile():
            t = nxt("vt", 16)
            return v32f[:, t * 528:(t + 1) * 528], [vtb[t]]

        def stile():
            k = nxt("st", 6)
            ap = hid2[:, 3 * k * 512:3 * k * 512 + 1536].bitcast(F32)
            return ap, [hidb[(3 * k + q) // 2][(3 * k + q) % 2] for q in range(3)]

        P.op("sp", lambda e: e.dma_start(out=vecs[:, :], in_=vecs_d[:, :]), writes=[vecsb], sem="vec", inc=16)
        P.op("pool", lambda e: e.memset(ones_bf[:, :], 1.0 / D), writes=[constb])
        P.op("pool", lambda e: e.memset(ones_f[:, :], 1.0 / D), writes=[constb])
        P.op("pool", lambda e: e.memset(one11[:, :], 1.0), writes=[constb])
        P.op("pool", lambda e: e.memset(small[:, 40:41], EPS), writes=[constb])

        def load_x(s, nt):
            for c in range(8):
                P.op("pool", lambda e: e.dma_start(out=xres[:, c, tsl(nt)],
                                                    in_=xT[c * 128:(c + 1) * 128, s * NT + nt * 512:s * NT + (nt + 1) * 512]),
                     writes=[xb[c][nt]], sem=f"xin{nt}", inc=16)
            v = P.cnt[f"xin{nt}"]
            for c in range(8):
                xb[c][nt].w = {f"xin{nt}": v}


        def pieces_of(i, key):
            a_, b_ = SLABS[i]["mreg" if key == "m" else "freg"]
            out = []
            pos = a_
            while pos < b_:
                out.append((pos, min(pos + CH, b_)))
                pos = out[-1][1]
            return out

        def convert_pieces(i, key, sel=None, deps=()):
            for p, (p0, p1) in enumerate(pieces_of(i, key)):
                if sel is not None and p not in sel:
                    continue
                b = cvb[(i, key, p)]
                P.op("pool", lambda e: e.dma_start(out=wbf[:, p0:p1], in_=wpack[:, p0:p1]),
                     after=list(deps), writes=[b], sem=cvsem[(i, key, p)], inc=16)

        def marker(eng):
            m = Buf("marker")
            if P.cnt.get(eng, 0) > 0:
                m.w = {eng: P.cnt[eng]}
            return m

        P.op("pool", lambda e: e.dma_start(out=ident_bf[:, :], in_=ident_d[:, :]), writes=[constb], sem="idn", inc=16)
        constb.w = {"idn": P.cnt["idn"], "pool": P.cnt["pool"]}
        convert_pieces(0, "m", sel=(0, 1))
        load_x(0, 0)
        load_x(0, 1)

        P.op("act", lambda e: e.activation(cact[:, :], vcol("ct", 0, 8), AF.Silu), reads=[vecsb], writes=[cactb])

        seq = _slab_order(nlayers, nsup)
        wstate = dict(loaded=0, used=0, done=0)

        def issue_loads():
            while wstate["loaded"] < len(seq) and wstate["loaded"] < wstate["done"] + NRING:
                g = wstate["loaded"]
                i, part, o, L = seq[g]
                k = g % NRING
                a_ = SLABS[i]["mreg" if part == "m" else "freg"][0]
                pcs = range((o - a_) // CH, (o + L - 1 - a_) // CH + 1)
                P.op("sp", lambda e: e.dma_start(out=wring[:, k, 0:L], in_=wbf[:, o:o + L]),
                     reads=[cvb[(i, part, p)] for p in pcs], writes=[wb[k]], sem=f"w{k}", inc=16)
                wstate["loaded"] += 1

        def next_slab(expect):
            g = wstate["used"]
            assert seq[g][0:2] == expect, (g, seq[g], expect)
            issue_loads()
            assert wstate["loaded"] > g, ("slab not loadable", g, wstate)
            wstate["used"] += 1
            k = g % NRING
            return wring[:, k, :], wb[k]

        NE = 24
        adastate = dict(loaded=0, n=0)
        stageb = lambda k: [vtb[4 * k + q] for q in range(4)]

        def stage(k):
            return v32f[:, k * 2112:k * 2112 + 2048]

        def ada_issue(i, upto):
            while adastate["loaded"] <= min(upto, NE - 1):
                e_ = adastate["loaded"]
                k = (adastate["n"] + e_) % 4
                src_ap = adaw[i].rearrange("(dc p) e -> p dc e", p=128)[:, :, e_ * 256:(e_ + 1) * 256]
                dst_ap = stage(k).rearrange("p (dc e) -> p dc e", dc=8)
                P.op("sp", lambda e: e.dma_start(out=dst_ap, in_=src_ap), writes=stageb(k), sem=f"ad{k}", inc=16)
                adastate["loaded"] += 1

        def mod_cast(i, e_):
            ada_issue(i, e_ + 3)
            k = (adastate["n"] + e_) % 4
            dk = e_ % 2
            P.op("dve", lambda e: e.tensor_copy(dg[:, dk, 0:2048], stage(k)), reads=stageb(k), writes=[dgb[dk]])

        def mod_mv(i, e_):
            dk = e_ % 2
            bk, bkb = bank()

            def mv(pe):
                for dc in range(8):
                    ins = pe.matmul(bk[0:1, 0:256], cact[:, dc:dc + 1], dg[:, dk, dc * 256:(dc + 1) * 256], start=(dc == 0), stop=(dc == 7))
                return ins

            P.op("pe", mv, reads=[dgb[dk], cactb], writes=[bkb])
            rk = e_ % 2
            P.op("act", lambda e: e.activation(rows_t[0:1, rk, 0:256], bk[0:1, 0:256], AF.Copy), reads=[bkb], writes=[rowsb[rk]])

        def mod_tr(i, e_):
            rk = e_ % 2
            bk2, bkb2 = bank()

            def tr(pe):
                for q in range(2):
                    ins = pe.matmul(bk2[:, q:q + 1], rows_t[0:1, rk, q * 128:(q + 1) * 128], one11[0:1, 0:1], start=True, stop=True)
                return ins

            P.op("pe", tr, reads=[rowsb[rk], constb], writes=[bkb2])
            P.op("dve", lambda e: e.tensor_tensor(modT[:, i, e_ * 2:(e_ + 1) * 2], bk2[:, 0:2], vcol(f"adab{i}", e_ * 2, 2), ALU.add),
                 reads=[bkb2, vecsb], writes=[modb[i]])

        def mod_step(i, t):
            if t == 0:
                adastate["loaded"] = 0
            if t < NE:
                mod_cast(i, t)
            if 1 <= t <= NE:
                mod_mv(i, t - 1)
            if 2 <= t <= NE + 1:
                mod_tr(i, t - 2)
            if t == NE + 1:
                adastate["n"] += NE
                mod_finish(i)

        def mod_finish(i):
            kind = i % 3
            P.op("dve", lambda e: e.scalar_tensor_tensor(der[:, i, 0:8], modT[:, i, 8:16], 1.0, vcol(f"nmg{i}", 0, 8),
                                                          ALU.add, ALU.mult), reads=[modb[i], vecsb], writes=[derb[i]])
            P.op("dve", lambda e: e.scalar_tensor_tensor(der[:, i, 8:16], modT[:, i, 32:40], 1.0, vcol(f"nfg{i}", 0, 8),
                                                          ALU.add, ALU.mult), reads=[modb[i], vecsb], writes=[derb[i]])
            if kind == 2:
                P.op("dve", lambda e: e.tensor_tensor(der[:, i, 16:24], modT[:, i, 16:24], vcol("psc", 0, 8), ALU.mult),
                     reads=[modb[i], vecsb, derb[i]], writes=[derb[i]])
            if kind == 1:
                P.op("dve", lambda e: e.tensor_tensor(der[:, i, 16:24], modT[:, i, 16:24], vcol("bb2", 0, 8), ALU.mult),
                     reads=[modb[i], vecsb, derb[i]], writes=[derb[i]])

        for t in range(NE + 2):
            mod_step(0, t)
            if t == 12:
                convert_pieces(0, "m", sel=(2, 3), deps=[marker("dve")])
        convert_pieces(0, "f", deps=[marker("dve")])

        units = []

        def unit(fn, nslabs=0):
            units.append(dict(emit=fn, nslabs=nslabs, st={}))

        pend_stats = {0: [], 1: []}
        sqctr = [0, 0]

        def flush_stats(nt):
            while pend_stats[nt]:
                pend_stats[nt].pop(0)()

        def stats_chunk(c, nt, defer=True):
            bk, bkb = statbank(nt)
            k = sqctr[nt] % 2
            sqctr[nt] += 1
            P.op("act", lambda e: e.activation(sqr2[:, nt, k, :], xres[:, c, tsl(nt)], AF.Square), reads=[xb[c][nt]], writes=[sq2b[nt][k]])

            def mm():
                P.op("pe", lambda e: e.matmul(bk, ones_bf[:, :], sqr2[:, nt, k, :], start=(c == 0), stop=(c == 7)),
                     reads=[sq2b[nt][k], constb], writes=[bkb])

            if defer:
                pend_stats[nt].append(mm)
            else:
                mm()

        def U_stats():
            def emit(nt, st):
                for c in range(8):
                    stats_chunk(c, nt, defer=False)
            unit(emit)

        def U_norm(gs, sh, mode, deps, s=0):
            def chunks(nt, st, c0):
                rk = st[("rk", nt)]
                for c in range(c0, c0 + 4):
                    tk = nxt("tmp", 4)
                    P.op("dve", lambda e: e.scalar_tensor_tensor(
                        tmp_t[:, tk, :], xres[:, c, tsl(nt)], gs[:, c:c + 1], rstd_t[:, rk, :], ALU.mult, ALU.mult),
                        reads=[xb[c][nt], rstdb[rk]] + deps, writes=[tmpb[tk]])
                    if mode == "out":
                        P.op("sp", lambda e: e.dma_start(
                            out=outT[c * 128:(c + 1) * 128, s * NT + nt * 512: s * NT + (nt + 1) * 512], in_=tmp_t[:, tk, :]),
                            reads=[tmpb[tk]], sem=f"ot{tk}", inc=16)
                    elif mode == "h":
                        P.op("act", lambda e: e.activation(hbuf[:, c, tsl(nt)], tmp_t[:, tk, :], AF.Identity,
                                                            bias=sh[:, c:c + 1], scale=1.0),
                             reads=[tmpb[tk]] + deps, writes=[hb[c][nt]])
                    else:
                        P.op("act", lambda e: e.activation(v32f[:, (c * 2 + nt) * 528 + 15:(c * 2 + nt) * 528 + 527], tmp_t[:, tk, :], AF.Identity,
                                                            bias=sh[:, c:c + 1], scale=1.0),
                             reads=[tmpb[tk]] + deps, writes=[vtb[c * 2 + nt]])

            def emit_a(nt, st):
                flush_stats(nt)
                bk, bkb = statbank(nt)
                rk = nt
                st[("rk", nt)] = rk
                P.op("act", lambda e: e.activation(rstd_t[:, rk, :], bk, AF.Sqrt, bias=small[:, 40:41], scale=1.0),
                     reads=[bkb, constb], writes=[rstdb[rk]])
                P.op("dve", lambda e: e.reciprocal(rstd_t[:, rk, :], rstd_t[:, rk, :]), reads=[rstdb[rk]], writes=[rstdb[rk]])
                chunks(nt, st, 0)

            u0 = dict(emit=emit_a, nslabs=0, st={})
            units.append(u0)
            units.append(dict(emit=lambda nt, st: chunks(nt, u0["st"], 4), nslabs=0, st={}))

        def resid_update(bk, bkb, c, nt, gvec, deps, bias=None):
            flush_stats(nt)
            P.op("dve", lambda e: e.scalar_tensor_tensor(xres[:, c, tsl(nt)], bk, gvec, xres[:, c, tsl(nt)], ALU.mult, ALU.add),
                 reads=[bkb] + deps, writes=[xb[c][nt]])
            if bias is not None:
                bap, bdeps = bias
                P.op("dve", lambda e: e.tensor_scalar(xres[:, c, tsl(nt)], xres[:, c, tsl(nt)], bap, None, ALU.add),
                     reads=[xb[c][nt]] + bdeps, writes=[xb[c][nt]])
            stats_chunk(c, nt)

        def proj_group(slab, slabb, col0, kchunks, stride, rhs_fn, rhs_bufs, nt):
            bk, bkb = bank()

            def fn(pe):
                for k in range(kchunks):
                    ins = pe.matmul(bk, slab[:, k * stride + col0:k * stride + col0 + 128], rhs_fn(k, nt),
                                    start=(k == 0), stop=(k == kchunks - 1))
                return ins

            P.op("pe", fn, reads=[slabb] + [rhs_bufs[k][nt] for k in range(kchunks)], writes=[bkb])
            return bk, bkb

        h_rhs = lambda k, nt: hbuf[:, k, tsl(nt)]
        hid_rhs = lambda k, nt: hid(k, nt)

        def get_slab(nt, st, expect):
            if nt == 0:
                st["slab"] = next_slab(expect)
            return st["slab"]

        def units_A(i, s):
            la = i // 3
            g1 = modT[:, i, 16:24]

            def mk_in(j):
                def emit(nt, st):
                    slab, slabb = get_slab(nt, st, (i, "m"))
                    U, Ub = vtile()
                    if s == 0 and nt == 0:
                        P.op("dve", lambda e: e.memset(U[:, 0:2], 0.0), writes=Ub)
                    else:
                        P.op("dve", lambda e: e.tensor_copy(U[:, 0:2], haloA[:, la, j, :]), reads=[haloAb[la][j]], writes=Ub)
                    gc, gcb = proj_group(slab, slabb, 0, 8, 384, h_rhs, hb, nt)
                    hv, hvb = proj_group(slab, slabb, 128, 8, 384, h_rhs, hb, nt)
                    gb, gbb = proj_group(slab, slabb, 256, 8, 384, h_rhs, hb, nt)
                    k = nxt("sg", 2)
                    P.op("act", lambda e: e.activation(sg_t[:, k, :], hv, AF.Copy), reads=[hvb], writes=[sgb[k]])
                    P.op("dve", lambda e: e.tensor_tensor(U[:, 2:514], gc, sg_t[:, k, :], ALU.mult), reads=[gcb, sgb[k]], writes=Ub)
                    tk = nxt("tmp", 4)
                    P.op("act", lambda e: e.activation(tmp_t[:, tk, :], gb, AF.Copy), reads=[gbb], writes=[tmpb[tk]])
                    P.op("dve", lambda e: e.tensor_copy(haloA[:, la, j, :], U[:, 512:514]), reads=Ub, writes=[haloAb[la][j]])
                    V, Vb = vtile()
                    w = lambda kk: vcol(f"acw{la}", kk * 8 + j, 1)
                    P.op("dve", lambda e: e.tensor_scalar(V[:, 0:512], U[:, 0:512], w(0), None, ALU.mult), reads=Ub + [vecsb], writes=Vb)
                    P.op("dve", lambda e: e.scalar_tensor_tensor(V[:, 0:512], U[:, 1:513], w(1), V[:, 0:512], ALU.mult, ALU.add),
                         reads=Ub + Vb, writes=Vb)
                    P.op("dve", lambda e: e.scalar_tensor_tensor(V[:, 0:512], U[:, 2:514], w(2), V[:, 0:512], ALU.mult, ALU.add),
                         reads=Ub + Vb, writes=Vb)
                    P.op("dve", lambda e: e.tensor_tensor(hid(j, nt), tmp_t[:, tk, :], V[:, 0:512], ALU.mult),
                         reads=[tmpb[tk]] + Vb, writes=[hidb[j][nt]])
                unit(emit, 1)

            def mk_out(m):
                def emit(nt, st):
                    slab, slabb = get_slab(nt, st, (i, "m"))
                    grp = [proj_group(slab, slabb, q * 128, 8, 512, hid_rhs, hidb, nt) for q in range(4)]
                    for q in range(4):
                        c = m * 4 + q
                        resid_update(grp[q][0], grp[q][1], c, nt, g1[:, c:c + 1], [modb[i]])
                unit(emit, 1)

            for j in range(8):
                mk_in(j)
            for m in range(2):
                mk_out(m)

        def units_B(i, s):
            g1 = modT[:, i, 16:24]

            def mk_in(m):
                def emit(nt, st):
                    slab, slabb = get_slab(nt, st, (i, "m"))
                    convs = []
                    for q in range(2):
                        j = 2 * m + q
                        uk = nxt("ubf", 3)
                        U = ubf[:, uk, :]
                        if s == 0 and nt == 0:
                            P.op("dve", lambda e: e.memset(U[:, 0:30], 0.0), writes=[ubfb[uk]])
                        else:
                            P.op("dve", lambda e: e.tensor_copy(U[:, 0:30], haloB[:, j, :]), reads=[haloBb[j]], writes=[ubfb[uk]])
                        dk = nxt("dg", 2)

                        def mkdiag(e):
                            for kk in range(31):
                                ins = e.tensor_scalar(dg[:, dk, kk * 128:(kk + 1) * 128], ident_bf[:, :], vcol("bcw", kk * 8 + j, 1), None, ALU.mult)
                            return ins

                        P.op("dve", mkdiag, reads=[vecsb, constb], writes=[dgb[dk]])
                        a, ab = proj_group(slab, slabb, q * 256, 8, 512, h_rhs, hb, nt)
                        gt, gtb = proj_group(slab, slabb, q * 256 + 128, 8, 512, h_rhs, hb, nt)
                        k = nxt("sg", 2)
                        P.op("act", lambda e: e.activation(sg_t[:, k, :], gt, AF.Sigmoid, bias=vcol("bb1", 8 + j, 1), scale=1.0),
                             reads=[gtb, vecsb], writes=[sgb[k]])
                        P.op("dve", lambda e: e.scalar_tensor_tensor(U[:, 30:542], a, vcol("bb1", j, 1), sg_t[:, k, :], ALU.add, ALU.mult),
                             reads=[ab, sgb[k], vecsb], writes=[ubfb[uk]])
                        P.op("dve", lambda e: e.tensor_copy(haloB[:, j, :], U[:, 512:542]), reads=[ubfb[uk]], writes=[haloBb[j]])
                        convs.append((j, uk, dk, U))
                    for (j, uk, dk, U) in convs:
                        bk, bkb = bank()

                        def conv(pe):
                            for kk in range(31):
                                ins = pe.matmul(bk, dg[:, dk, kk * 128:(kk + 1) * 128], U[:, kk:kk + 512], start=(kk == 0), stop=(kk == 30))
                            return ins

                        P.op("pe", conv, reads=[dgb[dk], ubfb[uk]], writes=[bkb])
                        P.op("act", lambda e: e.activation(v32f[:, (j * 2 + nt) * 528 + 0:(j * 2 + nt) * 528 + 512], bk, AF.Identity, bias=vcol("bcb", j, 1), scale=1.0),
                             reads=[bkb, vecsb], writes=[vtb[j * 2 + nt]])
                unit(emit, 1)

            def ln_chunks(nt, st, c0):
                k, rk = st[("ln", nt)]
                mus = sg_t[:, k, :]
                R = rstd_t[:, rk, :]
                for c in range(c0, c0 + 4):
                    tk = nxt("tmp", 4)
                    T = tmp_t[:, tk, :]
                    P.op("dve", lambda e: e.tensor_tensor(T, v32f[:, (c * 2 + nt) * 528 + 0:(c * 2 + nt) * 528 + 512], mus, ALU.subtract),
                         reads=[vtb[c * 2 + nt], sgb[k]], writes=[tmpb[tk]])
                    P.op("dve", lambda e: e.tensor_tensor(T, T, R, ALU.mult), reads=[tmpb[tk], rstdb[rk]], writes=[tmpb[tk]])
                    P.op("act", lambda e: e.activation(hid(c, nt), T, AF.Silu, bias=vcol("blb", c, 1), scale=vcol("blg", c, 1)),
                         reads=[tmpb[tk], vecsb], writes=[hidb[c][nt]])

            def emit_ln_a(nt, st):
                flush_stats(nt)
                mu, mub = bank()
                ms, msb = bank()
                sqs = []
                for c in range(8):
                    k = c % 2
                    if c >= 2:
                        sqs.pop(0)()
                    P.op("act", lambda e: e.activation(sqr2[:, nt, k, :], v32f[:, (c * 2 + nt) * 528 + 0:(c * 2 + nt) * 528 + 512], AF.Square),
                         reads=[vtb[c * 2 + nt]], writes=[sq2b[nt][k]])
                    P.op("pe", lambda e: e.matmul(mu, ones_f[:, :], v32f[:, (c * 2 + nt) * 528 + 0:(c * 2 + nt) * 528 + 512], start=(c == 0), stop=(c == 7)),
                         reads=[vtb[c * 2 + nt], constb], writes=[mub])

                    def msmm(c=c, k=k):
                        P.op("pe", lambda e: e.matmul(ms, ones_bf[:, :], sqr2[:, nt, k, :], start=(c == 0), stop=(c == 7)),
                             reads=[sq2b[nt][k], constb], writes=[msb])

                    sqs.append(msmm)
                while sqs:
                    sqs.pop(0)()
                k = nxt("sg", 2)
                mus = sg_t[:, k, :]
                P.op("act", lambda e: e.activation(mus, mu, AF.Copy), reads=[mub], writes=[sgb[k]])
                rk = nt
                R = rstd_t[:, rk, :]
                P.op("dve", lambda e: e.tensor_tensor(R, mus, mus, ALU.mult), reads=[sgb[k]], writes=[rstdb[rk]])
                P.op("dve", lambda e: e.tensor_tensor(R, ms, R, ALU.subtract), reads=[msb, rstdb[rk]], writes=[rstdb[rk]])
                P.op("act", lambda e: e.activation(R, R, AF.Sqrt, bias=small[:, 40:41], scale=1.0),
                     reads=[rstdb[rk], constb], writes=[rstdb[rk]])
                P.op("dve", lambda e: e.reciprocal(R, R), reads=[rstdb[rk]], writes=[rstdb[rk]])
                st[("ln", nt)] = (k, rk)
                ln_chunks(nt, st, 0)
                ln_chunks(nt, st, 4)

            def mk_out(m):
                def emit(nt, st):
                    slab, slabb = get_slab(nt, st, (i, "m"))
                    grp = [proj_group(slab, slabb, q * 128, 8, 512, hid_rhs, hidb, nt) for q in range(4)]
                    for q in range(4):
                        c = m * 4 + q
                        resid_update(grp[q][0], grp[q][1], c, nt, g1[:, c:c + 1], [modb[i]], bias=(der[:, i, 16 + c:17 + c], [derb[i]]))
                unit(emit, 1)

            for m in range(4):
                mk_in(m)
            unit(emit_ln_a)
            for m in range(2):
                mk_out(m)

        def units_C(i, s):
            def emit(nt, st):
                slab, slabb = get_slab(nt, st, (i, "m"))
                for c in range(8):
                    g = c // 2
                    w = 2 << g
                    t = c * 2 + nt
                    H = v32f[:, t * 528:(t + 1) * 528]
                    Hb = [vtb[t]]
                    engname = "dve"
                    if s == 0 and nt == 0:
                        P.op("dve", lambda e: e.memset(H[:, 0:15], 0.0), writes=Hb)
                    else:
                        P.op("dve", lambda e: e.tensor_copy(H[:, 0:15], haloC[:, c, :]), reads=[haloCb[c]], writes=Hb)
                    P.op("dve", lambda e: e.tensor_copy(haloC[:, c, :], H[:, 512:527]), reads=Hb, writes=[haloCb[c]])
                    prev, prevb = H, Hb
                    for l in range(1, g + 2):
                        a = 15 - (w - (1 << l))
                        sh_ = 1 << (l - 1)
                        Sx, Sb = stile()
                        P.op(engname, lambda e: e.tensor_tensor(Sx[:, a:527], prev[:, a:527], prev[:, a - sh_:527 - sh_], ALU.add),
                             reads=prevb, writes=Sb)
                        prev, prevb = Sx, Sb
                    P.op("dve", lambda e: e.scalar_tensor_tensor(hbuf[:, c, tsl(nt)], prev[:, 15:527], 1.0 / w, H[:, 15:527],
                                                                  ALU.mult, ALU.subtract),
                         reads=prevb + Hb, writes=[hb[c][nt]])
                    if s == 0 and nt == 0:
                        P.op("dve", lambda e: e.tensor_tensor(small[:, 0:16], prev[:, 15:31], vcol("invcnt", g * 16, 16), ALU.mult),
                             reads=prevb + [vecsb, smallb], writes=[smallb])
                        P.op("dve", lambda e: e.tensor_tensor(hbuf[:, c, 0:16], small[:, 0:16], H[:, 15:31], ALU.subtract),
                             reads=[smallb] + Hb, writes=[hb[c][0]])
                for half in range(2):
                    grp = []
                    for g in range(2 * half, 2 * half + 2):
                        for m in range(2):
                            bk, bkb = bank()

                            def fn(pe):
                                for kc in range(2):
                                    o = (g * 2 + kc) * 256 + m * 128
                                    ins = pe.matmul(bk, slab[:, o:o + 128], hbuf[:, 2 * g + kc, tsl(nt)], start=(kc == 0), stop=(kc == 1))
                                return ins

                            P.op("pe", fn, reads=[slabb, hb[2 * g][nt], hb[2 * g + 1][nt]], writes=[bkb])
                            grp.append((2 * g + m, bk, bkb))
                    for (c, bk, bkb) in grp:
                        resid_update(bk, bkb, c, nt, der[:, i, 16 + c:17 + c], [derb[i]])
            unit(emit, 1)

        def units_ffn(i, s):
            g2 = modT[:, i, 40:48]
            domod = (s == 0 and i + 1 < nlayers)

            def mk_mod(t):
                def emit(nt, st):
                    if nt == 0:
                        mod_step(i + 1, t)
                unit(emit, 0)

            def mk_in(m):
                def emit(nt, st):
                    slab, slabb = get_slab(nt, st, (i, "f"))
                    for q in range(2):
                        f = 2 * m + q
                        gt, gtb = proj_group(slab, slabb, q * 256, 8, 512, h_rhs, hb, nt)
                        up, upb = proj_group(slab, slabb, q * 256 + 128, 8, 512, h_rhs, hb, nt)
                        k = nxt("sg", 2)
                        P.op("act", lambda e: e.activation(sg_t[:, k, :], gt, AF.Silu), reads=[gtb], writes=[sgb[k]])
                        P.op("dve", lambda e: e.tensor_tensor(hid(f, nt), up, sg_t[:, k, :], ALU.mult),
                             reads=[upb, sgb[k]], writes=[hidb[f][nt]])
                unit(emit, 1)

            def mk_out(c):
                def emit(nt, st):
                    slab, slabb = get_slab(nt, st, (i, "f"))
                    bk, bkb = proj_group(slab, slabb, 0, FC, 128, hid_rhs, hidb, nt)
                    resid_update(bk, bkb, c, nt, g2[:, c:c + 1], [modb[i]])
                unit(emit, 1)

            tq = list(range(NE + 2)) if domod else []
            for m in range(11):
                mk_in(m)
                for _ in range(2):
                    if tq:
                        mk_mod(tq.pop(0))
            for c in range(8):
                mk_out(c)
                if tq:
                    mk_mod(tq.pop(0))
            assert not tq

        for s in range(nsup):
            if s > 0:
                unit(lambda nt, st, s=s: load_x(s, nt))
            U_stats()
            for i in range(nlayers):
                kind = i % 3
                deps = [derb[i], modb[i]]
                U_norm(der[:, i, 0:8], modT[:, i, 0:8], "h32" if kind == 2 else "h", deps)
                n_before = len(units)
                if kind == 0:
                    units_A(i, s)
                elif kind == 1:
                    units_B(i, s)
                else:
                    units_C(i, s)
                if s == 0 and i + 1 < nlayers:
                    def cv_emit(nt, st, i2=i + 1):
                        if nt == 0:
                            mk = marker("pe")
                            convert_pieces(i2, "m", deps=[mk])
                            convert_pieces(i2, "f", deps=[mk])
                    units.insert(n_before + 1, dict(emit=cv_emit, nslabs=0, st={}))
                U_norm(der[:, i, 8:16], modT[:, i, 24:32], "h", deps)
                units_ffn(i, s)
            U_norm(vecs[:, VOFF["fg"]:VOFF["fg"] + 8], None, "out", [vecsb], s)

        for idx in range(len(units) + LAG):
            if idx < len(units):
                u = units[idx]
                u["emit"](0, u["st"])
            j = idx - LAG
            if j >= 0:
                u = units[j]
                u["emit"](1, u["st"])
                wstate["done"] += u["nslabs"]
                issue_loads()
        assert wstate["used"] == len(seq), (wstate, len(seq))
        P.q["sp"].append(([(f"ot{tk}", P.cnt[f"ot{tk}"]) for tk in range(4) if P.cnt.get(f"ot{tk}", 0) > 0], None, None, 0))
        _emit(P, nc, sems)
    return nc


def _emit(P, nc, sems):
    with nc.Block() as block:
        def make(q):
            def body(eng):
                for waits, fn, key, inc in q:
                    for k, v in waits:
                        eng.wait_ge(sems[k], v)
                    if fn is None:
                        continue
                    ins = fn(eng)
                    ins.then_inc(sems[key], inc)
            return body

        block.tensor(make(P.q["pe"]))
        block.scalar(make(P.q["act"]))
        block.vector(make(P.q["dve"]))
        block.gpsimd(make(P.q["pool"]))
        block.sync(make(P.q["sp"]))


_CACHE = {}


def kernel(**inputs):
    inp = {k: np.asarray(v) for k, v in inputs.items()}
    nlayers = NLAYERS
    wpack = _pack_weights(inp)
    adaw = np.ascontiguousarray(inp["ada_w"], dtype=np.float32)
    in_maps = []
    for b in range(8):
        in_maps.append({
            "xT": np.ascontiguousarray(inp["x"][b].T, dtype=np.float32),
            "vecs": _pack_vecs(inp, b),
            "adaw": adaw,
            "wpack": wpack,
            "ident": np.eye(128, dtype=np.float32),
        })
    nc = build(nlayers=nlayers)
    res = run_bass_kernel_spmd(nc, in_maps, core_ids=list(range(8)))
    out = np.stack([np.ascontiguousarray(np.asarray(r["outT"]).T) for r in res.results], axis=0)
    return out.astype(np.float32)
```
